# Optimizing a Trainium2 kernel written in Bass

```python
import jax, jax.numpy as jnp
from jax import lax
import numpy as np

D_MODEL = 1024
BATCH = 2
SEQ = 16384
DEPTH = 1
DEC_BATCH = 32
DEC_SEQ = 32
PAST_LEN = 4096

CHUNK = 64
BAND_CHUNKS = 8
ATT_BAND = BAND_CHUNKS * CHUNK
A_HEADS = 8
A_DH = 64
A_WIDTH = A_HEADS * A_DH
M_HEADS = 4
M_DH = 128
M_WIDTH = M_HEADS * M_DH
D_MIX = A_WIDTH + M_WIDTH
REL_CLIP = 128
N_REL = 2 * REL_CLIP + 1
CONV_W = 4
D_FF = 2816
EPS = 1e-6
NEG = -1e30

OFF_AK = A_WIDTH
OFF_AV = 2 * A_WIDTH
OFF_MQK = 3 * A_WIDTH
OFF_MV = OFF_MQK + 2 * M_WIDTH
OFF_MO = OFF_MV + M_WIDTH
OFF_MG = OFF_MO + M_WIDTH
D_IN = OFF_MG + 2 * M_HEADS

kernel_name = 'hybrid_chunkband_attn_mlstm_stream_step'


def rmsnorm(x, g):
    xf = x.astype(jnp.float32)
    y = xf * lax.rsqrt(jnp.mean(xf * xf, axis=-1, keepdims=True) + EPS)
    return (y * g.astype(jnp.float32)).astype(x.dtype)


def head_rmsnorm(x, g, n_heads):
    B, L, W = x.shape
    y = rmsnorm(x.reshape(B, L, n_heads, W // n_heads), g.reshape(n_heads, W // n_heads))
    return y.reshape(B, L, W)


def ffn_half(x, g, w1, w3, w2):
    h = rmsnorm(x, g)
    return x + 0.5 * ((jax.nn.silu(h @ w1) * (h @ w3)) @ w2)


def causal_conv(u, buf, w, b):
    L = u.shape[1]
    full = jnp.concatenate([buf.astype(u.dtype), u], axis=1)
    y = b
    for j in range(CONV_W):
        y = y + full[:, j:j + L] * w[j]
    return y, full[:, full.shape[1] - (CONV_W - 1):]


def mixer_inputs(h, w_in, conv_w, conv_b, gate_bias, conv_buf):
    f32 = jnp.float32
    B, L = h.shape[:2]
    z = h @ w_in
    aq = z[..., :OFF_AK].reshape(B, L, A_HEADS, A_DH)
    ak = z[..., OFF_AK:OFF_AV].reshape(B, L, A_HEADS, A_DH)
    av = z[..., OFF_AV:OFF_MQK].reshape(B, L, A_HEADS, A_DH)
    qk, new_buf = causal_conv(z[..., OFF_MQK:OFF_MV], conv_buf, conv_w, conv_b)
    qk = jax.nn.silu(qk)
    to_heads = lambda t: t.reshape(B, L, M_HEADS, M_DH).transpose(0, 2, 1, 3).astype(f32)
    mq = to_heads(qk[..., :M_WIDTH])
    mk = to_heads(qk[..., M_WIDTH:]) * (M_DH ** -0.5)
    mv = to_heads(z[..., OFF_MV:OFF_MO])
    og = jax.nn.sigmoid(z[..., OFF_MO:OFF_MG])
    g = z[..., OFF_MG:D_IN].astype(f32) + gate_bias.astype(f32)
    ig = g[..., :M_HEADS].transpose(0, 2, 1)
    lf = jax.nn.log_sigmoid(g[..., M_HEADS:]).transpose(0, 2, 1)
    return (aq, ak, av), (mq, mk, mv, ig, lf), og, new_buf


def band_attention(q, k, v, qpos, kpos, kvalid, rel_bias):
    s = jnp.einsum('bqhd,bkhd->bhqk', q, k).astype(jnp.float32) * (A_DH ** -0.5)
    idx = jnp.clip(qpos[:, None] - kpos[None, :], -REL_CLIP, REL_CLIP) + REL_CLIP
    s = s + rel_bias[:, idx].astype(jnp.float32)[None]
    s = jnp.where(kvalid[None, None, None, :], s, NEG)
    p = jax.nn.softmax(s, axis=-1).astype(v.dtype)
    return jnp.einsum('bhqk,bkhd->bqhd', p, v)


def attn_prompt(q, k, v, rel_bias):
    B, S = q.shape[:2]
    nc = S // CHUNK
    pad = ((0, 0), (ATT_BAND, 0), (0, 0), (0, 0))
    kp = jnp.pad(k, pad)
    vp = jnp.pad(v, pad)
    qc = jnp.moveaxis(q.reshape(B, nc, CHUNK, A_HEADS, A_DH), 1, 0)
    band = ATT_BAND + CHUNK

    def one_chunk(args):
        c, qb = args
        start = c * CHUNK
        kb = lax.dynamic_slice_in_dim(kp, start, band, axis=1)
        vb = lax.dynamic_slice_in_dim(vp, start, band, axis=1)
        qpos = start + jnp.arange(CHUNK)
        kpos = start - ATT_BAND + jnp.arange(band)
        return band_attention(qb, kb, vb, qpos, kpos, kpos >= 0, rel_bias)

    out = lax.map(one_chunk, (jnp.arange(nc), qc))
    return jnp.moveaxis(out, 0, 1).reshape(B, S, A_WIDTH)


def attn_sample(q, k, v, ck, cv, rel_bias):
    Lc, T = ck.shape[1], q.shape[1]
    kk = jnp.concatenate([ck.astype(k.dtype), k], axis=1)
    vv = jnp.concatenate([cv.astype(v.dtype), v], axis=1)
    qpos = PAST_LEN + jnp.arange(T)
    kpos = PAST_LEN - Lc + jnp.arange(Lc + T)
    out = band_attention(q, kk, vv, qpos, kpos, jnp.ones((Lc + T,), bool), rel_bias)
    return out.reshape(q.shape[0], T, A_WIDTH)


def mlstm_chunk(carry, inp):
    C, n, m = carry
    q, k, v, ig, lf = inp
    L = q.shape[2]
    b = jnp.cumsum(lf, axis=-1)
    causal = jnp.tril(jnp.ones((L, L), bool))
    D = jnp.where(causal, b[..., :, None] - b[..., None, :] + ig[..., None, :], -jnp.inf)
    inter = b + m[..., None]
    mt = jnp.maximum(inter, jnp.max(D, axis=-1))
    Dw = jnp.exp(D - mt[..., None])
    iw = jnp.exp(inter - mt)
    s = jnp.einsum('bhtd,bhsd->bhts', q, k) * Dw
    num = iw[..., None] * jnp.einsum('bhtd,bhde->bhte', q, C) + jnp.einsum('bhts,bhse->bhte', s, v)
    den = iw * jnp.einsum('bhtd,bhd->bht', q, n) + jnp.sum(s, axis=-1)
    h = num / jnp.maximum(jnp.abs(den), jnp.exp(-mt))[..., None]
    bL = b[..., -1]
    m_new = mt[..., -1]
    wk = jnp.exp(bL[..., None] - b + ig - m_new[..., None])
    decay = jnp.exp(bL + m - m_new)
    C_new = decay[..., None, None] * C + jnp.einsum('bhs,bhsd,bhse->bhde', wk, k, v)
    n_new = decay[..., None] * n + jnp.einsum('bhs,bhsd->bhd', wk, k)
    return (C_new, n_new, m_new), h


def mlstm_prompt(mq, mk, mv, ig, lf):
    B, H, S, _ = mq.shape
    nc = S // CHUNK
    to_chunks = lambda t: jnp.moveaxis(t.reshape((B, H, nc, CHUNK) + t.shape[3:]), 2, 0)
    f32 = jnp.float32
    init = (jnp.zeros((B, H, M_DH, M_DH), f32), jnp.zeros((B, H, M_DH), f32), jnp.zeros((B, H), f32))
    carry, hs = lax.scan(mlstm_chunk, init, (to_chunks(mq), to_chunks(mk), to_chunks(mv), to_chunks(ig), to_chunks(lf)))
    h = jnp.moveaxis(hs, 0, 2).reshape(B, H, S, M_DH)
    return h, carry


def mixer_output(a_out, m_h, og, norm_m, w_out):
    B, L = a_out.shape[:2]
    m_h = m_h.transpose(0, 2, 1, 3).reshape(B, L, M_WIDTH).astype(a_out.dtype) * og
    m_h = head_rmsnorm(m_h, norm_m, M_HEADS)
    return jnp.concatenate([a_out, m_h], axis=-1) @ w_out


def setup_inputs(seed: int = 0) -> dict:
    key = jax.random.key(seed)
    ks = iter(jax.random.split(key, 40))
    f32 = jnp.float32

    def nrm(shape, scale):
        return jax.random.normal(next(ks), shape, f32) * scale

    def gain(shape):
        return 1.0 + nrm(shape, 0.02)

    att_cache = min(ATT_BAND, PAST_LEN)
    x_prompt = nrm((BATCH, SEQ, D_MODEL), 1.0)
    x_sample = nrm((DEC_BATCH, DEC_SEQ, D_MODEL), 1.0)
    cache_attn_k = nrm((DEPTH, DEC_BATCH, att_cache, A_HEADS, A_DH), 1.0)
    cache_attn_v = nrm((DEPTH, DEC_BATCH, att_cache, A_HEADS, A_DH), 1.0)
    state_mlstm_C = nrm((DEPTH, DEC_BATCH, M_HEADS, M_DH, M_DH), 0.3)
    state_mlstm_n = nrm((DEPTH, DEC_BATCH, M_HEADS, M_DH), 1.0)
    state_mlstm_m = nrm((DEPTH, DEC_BATCH, M_HEADS), 0.5)
    state_mlstm_conv = nrm((DEPTH, DEC_BATCH, CONV_W - 1, 2 * M_WIDTH), 1.0)
    fbias = jnp.linspace(3.0, 6.0, M_HEADS).astype(f32)
    gate_bias = jnp.concatenate([nrm((DEPTH, M_HEADS), 0.1), fbias + nrm((DEPTH, M_HEADS), 0.1)], axis=-1)
    return {
        'x_prompt': x_prompt,
        'x_sample': x_sample,
        'cache_attn_k': cache_attn_k,
        'cache_attn_v': cache_attn_v,
        'state_mlstm_C': state_mlstm_C,
        'state_mlstm_n': state_mlstm_n,
        'state_mlstm_m': state_mlstm_m,
        'state_mlstm_conv': state_mlstm_conv,
        'norm_ffn1': gain((DEPTH, D_MODEL)),
        'w1_ffn1': nrm((DEPTH, D_MODEL, D_FF), D_MODEL ** -0.5),
        'w3_ffn1': nrm((DEPTH, D_MODEL, D_FF), D_MODEL ** -0.5),
        'w2_ffn1': nrm((DEPTH, D_FF, D_MODEL), D_FF ** -0.5),
        'norm_mix': gain((DEPTH, D_MODEL)),
        'w_in': nrm((DEPTH, D_MODEL, D_IN), D_MODEL ** -0.5),
        'conv_w': nrm((DEPTH, CONV_W, 2 * M_WIDTH), CONV_W ** -0.5),
        'conv_b': nrm((DEPTH, 2 * M_WIDTH), 0.02),
        'gate_bias': gate_bias,
        'rel_bias': nrm((DEPTH, A_HEADS, N_REL), 0.2),
        'norm_mlstm_out': gain((DEPTH, M_WIDTH)),
        'w_out': nrm((DEPTH, D_MIX, D_MODEL), D_MIX ** -0.5),
        'norm_ffn2': gain((DEPTH, D_MODEL)),
        'w1_ffn2': nrm((DEPTH, D_MODEL, D_FF), D_MODEL ** -0.5),
        'w3_ffn2': nrm((DEPTH, D_MODEL, D_FF), D_MODEL ** -0.5),
        'w2_ffn2': nrm((DEPTH, D_FF, D_MODEL), D_FF ** -0.5),
        'norm_final': gain((D_MODEL,)),
    }


def reference(x_prompt, x_sample, cache_attn_k, cache_attn_v, state_mlstm_C, state_mlstm_n,
              state_mlstm_m, state_mlstm_conv, norm_ffn1, w1_ffn1, w3_ffn1, w2_ffn1,
              norm_mix, w_in, conv_w, conv_b, gate_bias, rel_bias, norm_mlstm_out, w_out,
              norm_ffn2, w1_ffn2, w3_ffn2, w2_ffn2, norm_final):
    f32 = jnp.float32
    B, S = x_prompt.shape[:2]
    keep = min(ATT_BAND, S)
    xp, xs = x_prompt, x_sample
    kp_l, vp_l, Cp_l, np_l, mp_l, cp_l = [], [], [], [], [], []
    ks_l, vs_l, Cs_l, ns_l, ms_l, cs_l = [], [], [], [], [], []
    for l in range(DEPTH):
        xp = ffn_half(xp, norm_ffn1[l], w1_ffn1[l], w3_ffn1[l], w2_ffn1[l])
        xs = ffn_half(xs, norm_ffn1[l], w1_ffn1[l], w3_ffn1[l], w2_ffn1[l])

        zero_buf = jnp.zeros((B, CONV_W - 1, 2 * M_WIDTH), xp.dtype)
        (aq, ak, av), mins, og, buf = mixer_inputs(rmsnorm(xp, norm_mix[l]), w_in[l], conv_w[l],
                                                   conv_b[l], gate_bias[l], zero_buf)
        a_out = attn_prompt(aq, ak, av, rel_bias[l])
        m_h, (C, n, m) = mlstm_prompt(*mins)
        xp = xp + mixer_output(a_out, m_h, og, norm_mlstm_out[l], w_out[l])
        kp_l.append(ak[:, S - keep:])
        vp_l.append(av[:, S - keep:])
        Cp_l.append(C)
        np_l.append(n)
        mp_l.append(m)
        cp_l.append(buf)

        (aq, ak, av), mins, og, buf = mixer_inputs(rmsnorm(xs, norm_mix[l]), w_in[l], conv_w[l],
                                                   conv_b[l], gate_bias[l], state_mlstm_conv[l])
        a_out = attn_sample(aq, ak, av, cache_attn_k[l], cache_attn_v[l], rel_bias[l])
        carry0 = (state_mlstm_C[l].astype(f32), state_mlstm_n[l].astype(f32), state_mlstm_m[l].astype(f32))
        (C, n, m), m_h = mlstm_chunk(carry0, mins)
        xs = xs + mixer_output(a_out, m_h, og, norm_mlstm_out[l], w_out[l])
        ks_l.append(ak)
        vs_l.append(av)
        Cs_l.append(C)
        ns_l.append(n)
        ms_l.append(m)
        cs_l.append(buf)

        xp = ffn_half(xp, norm_ffn2[l], w1_ffn2[l], w3_ffn2[l], w2_ffn2[l])
        xs = ffn_half(xs, norm_ffn2[l], w1_ffn2[l], w3_ffn2[l], w2_ffn2[l])

    y_prompt = rmsnorm(xp, norm_final)
    y_sample = rmsnorm(xs, norm_final)
    return (y_prompt, y_sample,
            jnp.stack(kp_l), jnp.stack(vp_l), jnp.stack(Cp_l), jnp.stack(np_l), jnp.stack(mp_l), jnp.stack(cp_l),
            jnp.stack(ks_l), jnp.stack(vs_l), jnp.stack(Cs_l), jnp.stack(ns_l), jnp.stack(ms_l), jnp.stack(cs_l))
```

```python
import numpy as np
import concourse.bass as bass
import concourse.mybir as mybir
from concourse.bass_utils import run_bass_kernel_spmd

F32 = mybir.dt.float32
BF16 = mybir.dt.bfloat16
AF = mybir.ActivationFunctionType
ALU = mybir.AluOpType
AX = mybir.AxisListType

D = 1024
DFF = 2816
NJ = DFF // 128
DIN = 3592
EPS = 1e-6
MASKV = -30000.0
NCORES = 8
OFF_AK, OFF_AV, OFF_MQK, OFF_MV, OFF_MO, OFF_MG = 512, 1024, 1536, 2560, 3072, 3584


class Prog:
    def __init__(self, nc):
        self.nc = nc
        self.ops = []
        self.last_w = {}
        self.readers = {}

    def op(self, eng, fn, r=(), w=(), dma=None, cc=False):
        i = len(self.ops)
        deps = set()
        for res in r:
            if res in self.last_w:
                deps.add(self.last_w[res])
        for res in w:
            if res in self.last_w:
                deps.add(self.last_w[res])
            for j in self.readers.get(res, ()):
                deps.add(j)
        deps.discard(i)
        self.ops.append(dict(eng=eng, fn=fn, deps=deps, dma=dma, signal=False, sig=None, cc=cc))
        for res in r:
            self.readers.setdefault(res, []).append(i)
        for res in w:
            self.last_w[res] = i
            self.readers[res] = []
        return i

    def barrier(self):
        last = {}
        for idx, o in enumerate(self.ops):
            if o['fn'] is None:
                continue
            key = ('dma', o['dma']) if o['dma'] is not None else ('eng', o['eng'])
            last[key] = idx
        deps = set(last.values())
        for eng in ('sp', 'act', 'dve', 'pool', 'pe'):
            self.ops.append(dict(eng=eng, fn=None, deps=set(deps), dma=None, signal=False, sig=None, cc=False))
        self.last_w = {}
        self.readers = {}

    def emit(self, final_deps):
        nc = self.nc
        ops = self.ops
        ops.append(dict(eng='sp', fn=None, deps=set(final_deps), dma=None, signal=False, sig=None, cc=False))
        for o in ops:
            keep = set()
            for j in o['deps']:
                pj = ops[j]
                if (pj['dma'] is None and o['dma'] is None and pj['eng'] == 'pe' and o['eng'] == 'pe'
                        and o['fn'] is not None):
                    continue
                keep.add(j)
                pj['signal'] = True
            o['deps'] = keep
        cnt = {}
        sems = {}
        for o in ops:
            if not o['signal']:
                continue
            if o['cc']:
                key, inc = ('cc', id(o)), 1
            else:
                key = ('dma', o['dma']) if o['dma'] is not None else ('eng', o['eng'])
                inc = 16 if o['dma'] is not None else 1
            cnt[key] = cnt.get(key, 0) + inc
            o['sig'] = (key, cnt[key], inc)
            if key not in sems:
                sems[key] = nc.alloc_semaphore("s%d" % len(sems))
        per_eng = {}
        for o in ops:
            per_eng.setdefault(o['eng'], []).append(o)

        def run(engname):
            def body(e):
                waited = {}
                for o in per_eng.get(engname, []):
                    need = {}
                    for j in o['deps']:
                        key, val, _ = ops[j]['sig']
                        if val > need.get(key, 0):
                            need[key] = val
                    for key, val in need.items():
                        if waited.get(key, 0) >= val:
                            continue
                        e.wait_ge(sems[key], val)
                        waited[key] = val
                    if o['fn'] is None:
                        continue
                    ins = o['fn'](e)
                    if o['signal']:
                        if o['cc']:
                            ins.then_inc(sems[o['sig'][0]])
                        else:
                            ins.then_inc(sems[o['sig'][0]], o['sig'][2])
            return body

        with nc.Block() as block:
            block.sync(run('sp'))
            block.scalar(run('act'))
            block.vector(run('dve'))
            block.gpsimd(run('pool'))
            block.tensor(run('pe'))
        return len(sems), max(cnt.values()) if cnt else 0


def fap(t, free):
    return bass.AP(t.tensor, t.offset, [list(t.ap[0])] + [list(f) for f in free])


def build(NP):
    nc = bass.Bass("TRN2", target_bir_lowering=False)
    P = Prog(nc)
    NPRE = 4 if MODE_X else 3 * NP
    TT = NPRE + NP + 4
    NOUT = NP + 4
    SR = 4 * 128 + 8

    def din(name, shape):
        return nc.dram_tensor(name, list(shape), F32, kind="ExternalInput")

    def dout(name, shape):
        return nc.dram_tensor(name, list(shape), F32, kind="ExternalOutput")

    xin = din("xin", [TT * 128, D])
    W = {}
    for nm, shp in [("w1a", [D, DFF]), ("w3a", [D, DFF]), ("w2a", [DFF, D]),
                    ("w1b", [D, DFF]), ("w3b", [D, DFF]), ("w2b", [DFF, D]),
                    ("w_in", [D, DIN]), ("w_out", [D, D])]:
        W[nm] = din(nm, shp)
    gains_d = din("gains", [4, D])
    normm_d = din("normm", [1, 512])
    cw_d = din("cw", [128, 8 * 4])
    cb_d = din("cb", [128, 8])
    gb_d = din("gb", [4, 2])
    relexp_d = din("relexp", [128, 8 * 640])
    vmask_d = din("vmask", [2, 128, 640])
    cmask_d = din("cmask", [128, 64])
    ident_d = din("ident", [128, 128])
    sel_d = din("sel", [4, 4 * 128])
    halom_d = din("halom", [128, 1])
    flg_d = din("flg", [128, 4])
    fle_d = din("fle", [128, 8])
    flm_d = din("flm", [128, 8])
    ck_d = din("ck", [4, 512, 512])
    cv_d = din("cv", [4, 512, 512])
    sC_d = din("sC", [4, 4, 128, 128])
    sn_d = din("sn", [4, 4, 128])
    sm_d = din("sm", [4, 4])
    sconv_d = din("sconv", [4, 3, D])

    y_o = dout("y", [NOUT * 128, D])
    kp_o = dout("kp", [512, 512])
    vp_o = dout("vp", [512, 512])
    Cp_o = dout("Cp", [4, 128, 128])
    np_o = dout("npo", [4, 128])
    mp_o = dout("mpo", [4, 1])
    convp_o = dout("convp", [3, D])
    ks_o = dout("ks", [4 * 32, 512])
    vs_o = dout("vs", [4 * 32, 512])
    Cs_o = dout("Cs", [4, 4, 128, 128])
    ns_o = dout("ns", [4, 4, 128])
    ms_o = dout("ms", [4, 4])
    convs_o = dout("convs", [4, 3, D])

    x1_d = nc.dram_tensor("x1s", [TT * 128, D], F32)
    x2_d = nc.dram_tensor("x2s", [NOUT * 128, D], F32)
    sum_in = nc.dram_tensor("sum_in", [SR, 132], F32)
    sum_all = nc.dram_tensor("sum_all", [NCORES * SR, 132], F32)

    SB_END = 229344
    ptr = [16512]

    def sb(name, shape, dt):
        sz = int(np.prod(shape[1:])) * (2 if dt == BF16 else 4)
        sz = (sz + 31) // 32 * 32
        t = nc.alloc_sbuf_tensor_at(name, list(shape), dt, offset=ptr[0])
        ptr[0] += sz
        assert ptr[0] <= SB_END, (name, ptr[0])
        return t

    gain_t = sb("gain", [128, D], F32)
    gainf_t = sb("gainf", [128, D], F32)
    identf = sb("identf", [128, 128], F32)
    identb = sb("identb", [128, 128], BF16)
    st = sb("st", [128, 8], F32)
    chalf = sb("chalf", [128, 8], F32)
    xts = [sb("xt%d" % i, [128, D], F32) for i in range(2)]
    hb = sb("hb", [128, D], BF16)
    junk = sb("junk", [128, D], BF16)
    hTg = sb("hTg", [128, 8, 512], BF16)
    hT = hTg[:, :, 0:128]
    arena0 = ptr[0]
    wA = sb("wA", [128, 8 * DFF], BF16)
    wB = sb("wB", [128, 8 * DFF], BF16)
    wC = sb("wC", [128, NJ * D], BF16)
    xts += [sb("xt%d" % i, [128, D], F32) for i in range(2, 6)]
    gff = sb("gff", [128, NJ, 512], BF16)
    sas = [sb("sa%d" % i, [128, 512], F32) for i in range(2)]
    arena_end = ptr[0]
    w1v = wA[:, :].rearrange("p (k n) -> p k n", k=8)
    w3v = wB[:, :].rearrange("p (k n) -> p k n", k=8)
    w2v = wC[:, :].rearrange("p (j n) -> p j n", j=NJ)
    optr = [arena0]

    def ov(name, shape, dt):
        sz = int(np.prod(shape[1:])) * (2 if dt == BF16 else 4)
        sz = (sz + 31) // 32 * 32
        t = nc.alloc_sbuf_tensor_at(name, list(shape), dt, offset=optr[0])
        optr[0] += sz
        assert optr[0] <= SB_END, (name, optr[0], SB_END)
        return t

    winv = ov("win", [128, 8, DIN], BF16)
    woutv = ov("wout", [128, 8, D], BF16)
    bias_t = ov("bias", [128, 8, 640], F32)
    vmask_b = ov("vmaskb", [128, 2, 640], BF16)
    KT = ov("KT", [128, 4, 640], BF16)
    VR = ov("VR", [128, 5, 512], BF16)
    qT = ov("qT", [128, 4, 128], BF16)
    uT = ov("uT", [128, 8, 131], F32)
    cacc = ov("cacc", [128, 8, 128], F32)
    uTb = ov("uTb", [128, 4, 131], F32)
    rowsGs = [ov("rowsG%d" % i, [4, 2, 128], F32) for i in range(2)]
    tmk = ov("tmk", [128, 1024], F32)
    mqkT = ov("mqkT", [128, 8, 128], BF16)
    ktms = [ov("ktm%d" % i, [128, 512], BF16) for i in range(2)]
    kwt = ov("kwt", [128, 4, 128], BF16)
    v1s = [ov("v1_%d" % i, [128, 4, 132], BF16) for i in range(3)]
    ogs = ov("ogs", [128, 512], F32)
    S_sbs = [ov("S_sb%d" % i, [128, 640], F32) for i in range(2)]
    P_sbs = [ov("P_sb%d" % i, [128, 640], BF16) for i in range(2)]
    PT_sbs = [ov("PT_sb%d" % i, [128, 640], BF16) for i in range(2)]
    aout = ov("aout", [128, 1024], BF16)
    mixT = ov("mixT", [128, 8, 128], BF16)
    Cst = ov("Cst", [128, 4, 132], F32)
    Cb = ov("Cb", [128, 4, 132], BF16)
    Ctmp = ov("Ctmp", [128, 4, 132], F32)
    qCs = ov("qCs", [128, 4, 132], F32)
    nd = ov("nd", [128, 4, 132], F32)
    hm = ov("hm", [128, 4, 128], F32)
    WT = ov("WT", [128, 4, 64], F32)
    sTt = ov("sT", [128, 4, 64], BF16)
    tmpW = ov("tmpW", [128, 4, 64], F32)
    colss = [ov("cols%d" % i, [128, 12], F32) for i in range(2)]
    rowsBs = [ov("rowsB%d" % i, [4, 2, 128], F32) for i in range(2)]
    cw_t = ov("cw_t", [128, 8, 4], F32)
    cb_t = ov("cb_t", [128, 8], F32)
    normm_t = ov("normm_t", [128, 512], F32)
    cmask_t = ov("cmask_t", [128, 64], F32)
    sel_t = ov("sel_t", [4, 4, 128], F32)
    ones4 = ov("ones4", [4, 128], F32)
    ckl = ov("ckl", [128, 4, 512], BF16)
    sml = ov("sml", [128, 32], F32)
    rows = ov("rows", [4, 12, 128], F32)
    rsm = ov("rsm", [4, 16], F32)
    dg = ov("dg", [4, 4], F32)
    halom_t = ov("halom_t", [128, 1], F32)
    flg_t = ov("flg_t", [128, 4], F32)
    fle_t = ov("fle_t", [128, 8], F32)
    flm_t = ov("flm_t", [128, 8], F32)
    if MODE_X:
        Csum = ov("Csum", [128, 4, 132], F32)
        scal = ov("scal", [128, 8, 4, 2], F32)
        fold = ov("fold", [128, 40], F32)
    gb_t = ov("gb_t", [4, 4], F32)
    sbuf_used = (ptr[0], optr[0], arena_end)

    psT = nc.alloc_psum_tensor("psT", [128, 1024], BF16)
    ps7 = nc.alloc_psum_tensor("ps7", [128, 3584], F32)
    psAB = ps7[:, 0:1024]
    psA = ps7[:, 0:512]
    psB = ps7[:, 512:1024]
    psY = ps7[:, 1024:1536]
    psS = ps7[:, 1536:2560]
    psN = ps7[:, 2560:3584]
    fA = [ps7[:, 0:512], ps7[:, 512:1024]]
    fB = [ps7[:, 1024:1536], ps7[:, 1536:2048]]
    fY = [ps7[:, 2048:2560], ps7[:, 2560:3072]]

    def hv(ps):
        return ps.rearrange("p (h c) -> p h c", h=4)

    def dma(q, out, in_, r, w, key, slow=False):
        if slow:
            return P.op(q, lambda e: e.dma_start(out=out, in_=in_, allow_slow_non_contiguous=True), r=r, w=w, dma=key)
        return P.op(q, lambda e: e.dma_start(out=out, in_=in_), r=r, w=w, dma=key)

    def act(out, in_, func, r, w, bias=None, scale=None, accum=None):
        kw_ = {}
        if bias is not None:
            kw_['bias'] = bias
        if scale is not None:
            kw_['scale'] = scale
        if accum is not None:
            kw_['accum_out'] = accum
        return P.op('act', lambda e: e.activation(out=out, in_=in_, func=func, **kw_), r=r, w=w)

    def tt(eng, out, in0, in1, op, r, w):
        return P.op(eng, lambda e: e.tensor_tensor(out=out, in0=in0, in1=in1, op=op), r=r, w=w)

    def ts(eng, out, in0, s1, s2, op0, op1, r, w):
        if op1 is None:
            return P.op(eng, lambda e: e.tensor_scalar(out=out, in0=in0, scalar1=s1, scalar2=None, op0=op0), r=r, w=w)
        return P.op(eng, lambda e: e.tensor_scalar(out=out, in0=in0, scalar1=s1, scalar2=s2, op0=op0, op1=op1), r=r, w=w)

    def stt(eng, out, in0, sc, in1, op0, op1, r, w):
        return P.op(eng, lambda e: e.scalar_tensor_tensor(out=out, in0=in0, scalar=sc, in1=in1, op0=op0, op1=op1), r=r, w=w)

    def mm(out, lhsT, rhs, start, stop, r, w):
        return P.op('pe', lambda e: e.matmul(out, lhsT, rhs, start=start, stop=stop), r=r, w=w)

    def tr(out, in_, ident, r, w):
        return P.op('pe', lambda e: e.transpose(out, in_, ident), r=r, w=w)

    def cp(eng, out, in_, r, w):
        if eng == 'act':
            return P.op('act', lambda e: e.copy(out=out, in_=in_), r=r, w=w)
        return P.op(eng, lambda e: e.tensor_copy(out=out, in_=in_), r=r, w=w)

    def memset(eng, ap, val, w, r=()):
        return P.op(eng, lambda e: e.memset(ap, val), r=r, w=w)

    def recip(ap, r, w):
        return P.op('dve', lambda e: e.reciprocal(out=ap, in_=ap), r=r, w=w)

    def reduce(out, in_, op, r, w):
        return P.op('dve', lambda e: e.tensor_reduce(out=out, in_=in_, axis=AX.X, op=op), r=r, w=w)

    out_ops = []
    memset('pool', chalf[:, 0:4], -0.5, ['chalf'])
    memset('pool', chalf[:, 4:8], -1.0, ['chalf'])

    dma('sp', identf[:, :], ident_d[:, :], (), ['identf'], 'identf')
    dma('pool', identb[:, :], ident_d[:, :], (), ['identb'], 'identb')
    dma('sp', gainf_t[:, :], bass.AP(gains_d, 3 * D, [[0, 128], [1, D]]), (), ['gainf'], 'gainf')

    def load_gain(idx):
        dma('sp', gain_t[:, :], bass.AP(gains_d, idx * D, [[0, 128], [1, D]]), (), ['gain'], 'gain')

    def rmsnorm_to_hT(xt, xres):
        memset('pool', st[:, 0:1], 0.0, ['ss'])
        act(hb[:, :], xt[:, :], AF.Square, [xres, 'ss'], ['ss', 'hb'], accum=st[:, 0:1])
        ts('dve', st[:, 1:2], st[:, 0:1], 1.0 / D, EPS, ALU.mult, ALU.add, ['ss'], ['rs'])
        tt('pool', st[:, 1:2], st[:, 1:2], chalf[:, 0:1], ALU.pow, ['rs', 'chalf'], ['rs'])
        stt('dve', hb[:, :], xt[:, :], st[:, 1:2], gain_t[:, :], ALU.mult, ALU.mult, [xres, 'rs', 'gain'], ['hb'])
        for c in range(8):
            tr(psT[:, c * 128:(c + 1) * 128], hb[:, c * 128:(c + 1) * 128], identb[:, :], ['hb', 'identb'], ['psT'])
        cp('act', hT[:, :, :], psT[:, :].rearrange("p (k t) -> p k t", k=8), ['psT'], ['hT'])

    def load_ffn_weights(n1, n3, n2):
        for half in range(2):
            dma('pool', w1v[:, 4 * half:4 * half + 4, :],
                W[n1][512 * half:512 * half + 512, :].rearrange("(k p) n -> p k n", p=128), (), ['wA'], 'wA%d' % half)
            dma('pool', w3v[:, 4 * half:4 * half + 4, :],
                W[n3][512 * half:512 * half + 512, :].rearrange("(k p) n -> p k n", p=128), (), ['wB'], 'wB%d' % half)
            dma('pool', w2v[:, 11 * half:11 * half + 11, :],
                W[n2][1408 * half:1408 * half + 1408, :].rearrange("(j p) n -> p j n", p=128), (), ['wC'], 'wC%d' % half)

    slotc = [0]
    pend = []
    NSLOT = [2]

    def xload(src, src_res):
        k = slotc[0] % NSLOT[0]
        slotc[0] += 1
        dma('sp', xts[k][:, :], src, [src_res], ['xt%d' % k], 'xt%dl' % k)
        pend.append(k)

    def run_jobs(jobs):
        if not jobs:
            return
        jobs[0][0]()
        for k, (ld, comp) in enumerate(jobs):
            if k + 1 < len(jobs):
                jobs[k + 1][0]()
            comp(pend.pop(0))

    def ffn_norm_pre(k):
        xt = xts[k]
        xres = 'xt%d' % k
        memset('pool', st[:, 0:1], 0.0, ['ss'])
        act(hb[:, :], xt[:, :], AF.Square, [xres, 'ss'], ['ss', 'hb'], accum=st[:, 0:1])
        ts('dve', st[:, 1:2], st[:, 0:1], 1.0 / D, EPS, ALU.mult, ALU.add, ['ss'], ['rs'])
        tt('pool', st[:, 1:2], st[:, 1:2], chalf[:, 0:1], ALU.pow, ['rs', 'chalf'], ['rs'])
        stt('dve', hb[:, :], xt[:, :], st[:, 1:2], gain_t[:, :], ALU.mult, ALU.mult, [xres, 'rs', 'gain'], ['hb'])

    def ffn_norm_post(t):
        for c in range(8):
            tr(psT[:, c * 128:(c + 1) * 128], hb[:, c * 128:(c + 1) * 128], identb[:, :], ['hb', 'identb'], ['psT'])
        cp('act', hTg[:, :, t * 128:(t + 1) * 128], psT[:, :].rearrange("p (k t) -> p k t", k=8), ['psT'], ['hTg%d' % t])

    def ffn_norm(k, t):
        ffn_norm_pre(k)
        ffn_norm_post(t)

    def ffn_ab():
        hres = ['hTg%d' % t for t in range(4)]
        for j in range(NJ):
            pa, pb = fA[j % 2], fB[j % 2]
            ra, rb_ = 'fA%d' % (j % 2), 'fB%d' % (j % 2)
            for kc in range(8):
                mm(pa, w1v[:, kc, j * 128:(j + 1) * 128], hTg[:, kc, :], kc == 0, kc == 7, ['wA'] + hres, [ra])
            for kc in range(8):
                mm(pb, w3v[:, kc, j * 128:(j + 1) * 128], hTg[:, kc, :], kc == 0, kc == 7, ['wB'] + hres, [rb_])
            sa_ = sas[j % 2]
            act(sa_[:, :], pa, AF.Silu, [ra], ['sa%d' % (j % 2)])
            tt('dve', gff[:, j, :], sa_[:, :], pb, ALU.mult, ['sa%d' % (j % 2), rb_], ['gff'])

    def ffn_out(k, t, dst, dst_res, final):
        xt = xts[k]
        xres = 'xt%d' % k
        for half in range(2):
            py = fY[half]
            ry = 'fY%d' % half
            for j in range(NJ):
                mm(py, gff[:, j, t * 128:(t + 1) * 128], w2v[:, j, half * 512:(half + 1) * 512], j == 0, j == NJ - 1,
                   ['gff', 'wC'], [ry])
            stt('dve', xt[:, half * 512:(half + 1) * 512], py, 0.5, xt[:, half * 512:(half + 1) * 512],
                ALU.mult, ALU.add, [ry, xres], [xres])
        if final:
            memset('pool', st[:, 2:3], 0.0, ['ss2'])
            act(junk[:, :], xt[:, :], AF.Square, [xres, 'ss2'], ['ss2', 'junk'], accum=st[:, 2:3])
            ts('dve', st[:, 3:4], st[:, 2:3], 1.0 / D, EPS, ALU.mult, ALU.add, ['ss2'], ['rs2'])
            tt('pool', st[:, 3:4], st[:, 3:4], chalf[:, 0:1], ALU.pow, ['rs2', 'chalf'], ['rs2'])
            stt('dve', xt[:, :], xt[:, :], st[:, 3:4], gainf_t[:, :], ALU.mult, ALU.mult, [xres, 'rs2', 'gainf'], [xres])
        return dma('pool', dst, xt[:, :], [xres], [dst_res], xres + 's')

    def ffn_phase(tiles, final, collect):
        assert len(tiles) % 4 == 0
        NSLOT[0] = 6
        slotc[0] = 0
        del pend[:]
        ng = len(tiles) // 4
        slots = {}

        def load1(g, t):
            src, sres, _, _ = tiles[4 * g + t]
            xload(src, sres)
            slots[(g, t)] = pend.pop(0)
        for t in range(4):
            load1(0, t)
        for t in range(4):
            ffn_norm(slots[(0, t)], t)
        for g in range(ng):
            ffn_ab()
            if g + 1 < ng:
                load1(g + 1, 0)
                load1(g + 1, 1)
            for t in range(4):
                _, _, dst, dres = tiles[4 * g + t]
                if g + 1 < ng:
                    ffn_norm_pre(slots[(g + 1, t)])
                o = ffn_out(slots[(g, t)], t, dst, dres, final)
                if collect is not None:
                    collect.append(o)
                if g + 1 < ng:
                    if t < 2:
                        load1(g + 1, t + 2)
                    ffn_norm_post(t)
        NSLOT[0] = 2
        slotc[0] = 0

    load_gain(0)
    load_ffn_weights("w1a", "w3a", "w2a")
    ffn_phase([(xin[i * 128:(i + 1) * 128, :], 'dram_xin', x1_d[i * 128:(i + 1) * 128, :], 'x1_%d' % i)
               for i in range(TT)], False, None)
    P.barrier()

    A = ['wA', 'wB', 'wC']
    load_gain(1)
    for half in range(2):
        dma('pool', winv[:, 4 * half:4 * half + 4, :],
            W["w_in"][512 * half:512 * half + 512, :].rearrange("(k p) n -> p k n", p=128), (), A, 'win%d' % half)
    dma('pool', woutv[:, :, :], W["w_out"][:, :].rearrange("(k p) n -> p k n", p=128), (), A, 'wout')
    dma('sp', bias_t[:, :, :], relexp_d[:, :].rearrange("p (h k) -> p h k", h=8), (), A, 'bias')
    dma('pool', vmask_b[:, :, :], vmask_d[:, :, :].rearrange("v p k -> p v k"), (), ['vmaskb'] + A, 'vmask')
    dma('sp', cw_t[:, :, :], cw_d[:, :].rearrange("p (b j) -> p b j", b=8), (), A, 'cw')
    dma('sp', cb_t[:, :], cb_d[:, :], (), A, 'cb')
    dma('sp', normm_t[:, :], bass.AP(normm_d, 0, [[0, 128], [1, 512]]), (), A, 'normm')
    dma('sp', cmask_t[:, :], cmask_d[:, :], (), A, 'cmask')
    dma('sp', sel_t[:, :, :], sel_d[:, :].rearrange("k (h m) -> k h m", h=4), (), A, 'sel')
    dma('sp', halom_t[:, :], halom_d[:, :], (), A, 'halom')
    dma('sp', flg_t[:, :], flg_d[:, :], (), A, 'flg')
    dma('sp', fle_t[:, :], fle_d[:, :], (), A, 'fle')
    dma('sp', flm_t[:, :], flm_d[:, :], (), A, 'flm')
    dma('sp', gb_t[:, 0:2], gb_d[:, :], (), A, 'gb')
    ts('dve', gb_t[:, 2:3], gb_t[:, 1:2], -1.0, None, ALU.mult, None, A, ['gbn'])
    for v1_ in v1s:
        memset('pool', v1_[:, :, :], 1.0, ['v1_0', 'v1_1', 'v1_2'], A)
    memset('pool', uT[:, :, :], 0.0, ['uT'], A)
    memset('pool', uTb[:, :, :], 0.0, ['uTk1'], A)
    memset('pool', KT[:, :, :], 0.0, ['KT'], A)
    memset('pool', VR[:, :, :], 0.0, ['VR'], A)
    memset('pool', aout[:, :], 0.0, ['aout_a', 'aout_m'], A)
    memset('pool', ones4[:, :], 1.0, ['ones4'], A)
    memset('pool', Cst[:, :, :], 0.0, ['Cst'], A)
    memset('pool', Cb[:, :, :], 0.0, ['Cb'], A)
    memset('pool', rsm[:, :], 0.0, ['rsm', 'm_in'], A)

    def project_fm(col0, nblk, evac):
        b0 = 0
        while b0 < nblk:
            nb = min(4, nblk - b0)
            for bb in range(nb):
                c0 = col0 + (b0 + bb) * 128
                for kc in range(8):
                    mm(psA[:, bb * 128:(bb + 1) * 128], winv[:, kc, c0:c0 + 128], hT[:, kc, :], kc == 0, kc == 7,
                       A + ['hT'], ['psA'])
            evac(b0, nb)
            b0 += nb

    def project_tm(col0, evac):
        for kc in range(8):
            mm(psY[:, :], hT[:, kc, :], winv[:, kc, col0:col0 + 512], kc == 0, kc == 7, A + ['hT'], ['psY'])
        evac()

    def conv_part(eng, b0, b1, tcol, uv=None, ru='uT'):
        nb = b1 - b0
        if uv is None:
            uv = uT[:, b0:b1, :]
        tv = tmk[:, tcol:tcol + nb * 128].rearrange("p (b t) -> p b t", b=nb)
        rt = 'tmk%d' % tcol
        rc = 'cacc%d' % b0
        if eng == 'dve':
            for bb in range(nb):
                b_ = b0 + bb
                ts('dve', cacc[:, b_, :], uv[:, bb, 0:128], cw_t[:, b_, 0:1], cb_t[:, b_:b_ + 1], ALU.mult, ALU.add,
                   [ru, 'cw', 'cb'], [rc])
                for j in range(1, 4):
                    stt('dve', cacc[:, b_, :], uv[:, bb, j:j + 128], cw_t[:, b_, j:j + 1], cacc[:, b_, :], ALU.mult,
                        ALU.add, [ru, 'cw', rc], [rc])
        else:
            tt(eng, cacc[:, b0:b1, :], uv[:, :, 0:128], fap(cw_t[:, b0:b1, 0:1], [[4, nb], [0, 128]]), ALU.mult,
               [ru, 'cw'], [rc])
            for j in range(1, 4):
                tt(eng, tv, uv[:, :, j:j + 128], fap(cw_t[:, b0:b1, j:j + 1], [[4, nb], [0, 128]]), ALU.mult,
                   [ru, 'cw'], [rt])
                tt(eng, cacc[:, b0:b1, :], cacc[:, b0:b1, :], tv, ALU.add, [rt, rc], [rc])
            tt(eng, cacc[:, b0:b1, :], cacc[:, b0:b1, :], fap(cb_t[:, b0:b1], [[1, nb], [0, 128]]), ALU.add,
               [rc, 'cb'], [rc])
        act(tv, cacc[:, b0:b1, :], AF.Exp, [rc], [rt], scale=-1.0)
        act(tv, tv, AF.Ln, [rt], [rt], bias=1.0)
        act(tv, tv, AF.Exp, [rt], [rt], scale=-1.0)
        tt(eng, mqkT[:, b0:b1, :], cacc[:, b0:b1, :], tv, ALU.mult, [rc, rt], ['mqkT'])

    def conv_silu(b0, b1):
        if b0 == 0:
            conv_part('pool', 0, 4, 0)
            conv_part('dve', 4, 8, 512)
        else:
            conv_part('dve', b0, b1, 512)
        kb0 = max(b0, 4)
        ts('dve', mqkT[:, kb0:b1, :], mqkT[:, kb0:b1, :], 128 ** -0.5, None, ALU.mult, None, ['mqkT'], ['mqkT'])
        cp('pool', uT[:, b0:b1, 0:3], uT[:, b0:b1, 128:131], ['uT'], ['uT'])

    R_IG, R_L, R_B, R_A, R_M, R_NM, R_E1, R_IW, R_T1, R_T2 = range(10)
    rr = lambda k: 'row%d' % k
    R_ = lambda k: rows[:, k, :]
    v3 = lambda k: rows[:, k, :].rearrange("h (c t) -> h c t", c=2)

    def scan(src, op):
        cur = src
        k = 0
        for d in (1, 2, 4, 8, 16, 32):
            o = (R_T1, R_T2)[k % 2]
            k += 1
            cp('dve', v3(o)[:, :, 0:d], v3(cur)[:, :, 0:d], [rr(cur)], [rr(o)])
            tt('dve', v3(o)[:, :, d:64], v3(cur)[:, :, d:64], v3(cur)[:, :, 0:64 - d], op, [rr(cur)], [rr(o)])
            cur = o
        return cur

    def gates_rows(nchunks, lastcol, par=0, part='all'):
        cols = colss[par]
        rowsB = rowsBs[par]
        rowsG = rowsGs[par]
        rG = 'rowsG%d' % par
        if part in ('all', 'A'):
            for half, g0 in enumerate([OFF_MG, OFF_MG + 4]):
                for kc in range(8):
                    mm(psY[0:4, half * 128:(half + 1) * 128], winv[:, kc, g0:g0 + 4], hT[:, kc, :], kc == 0, kc == 7,
                       A + ['hT'], ['psY'])
            act(rowsG[:, 0, :], psY[0:4, 0:128], AF.Identity, ['psY', 'gb'] + A, [rG], bias=gb_t[:, 0:1])
            act(R_(R_T1), psY[0:4, 128:256], AF.Exp, ['psY', 'gbn'], [rr(R_T1)], bias=gb_t[:, 2:3], scale=-1.0)
            act(rowsG[:, 1, :], R_(R_T1), AF.Ln, [rr(R_T1)], [rG], bias=1.0)
            if part == 'A':
                return
        cp('dve', R_(R_L), rowsG[:, 1, :], [rG], [rr(R_L)])
        cs = scan(R_L, ALU.add)
        ts('dve', R_(R_B), R_(cs), -1.0, None, ALU.mult, None, [rr(cs)], [rr(R_B)])
        tt('dve', R_(R_A), rowsG[:, 0, :], R_(cs), ALU.add, [rG, rr(cs)], [rr(R_A)])
        cm = scan(R_A, ALU.max)
        for c in range(nchunks):
            sl = slice(64 * c, 64 * c + 64)
            lc = 64 * c + lastcol
            cp('dve', rsm[:, 1 + c:2 + c], rsm[:, 8:9], ['m_in'], ['rsm'])
            ts('dve', rows[:, R_M, sl], rows[:, cm, sl], rsm[:, 8:9], None, ALU.max, None, [rr(cm), 'm_in'], [rr(R_M)])
            tt('dve', rsm[:, 8:9], rows[:, R_B, lc:lc + 1], rows[:, R_M, lc:lc + 1], ALU.add, [rr(R_B), rr(R_M)],
               ['m_in'])
            tt('dve', rsm[:, 9:10], rsm[:, 9:10], rows[:, R_B, lc:lc + 1], ALU.add, [rr(R_B), 'm_in'], ['m_in'])
            ts('dve', rows[:, R_NM, sl], rows[:, R_M, sl], -1.0, None, ALU.mult, None, [rr(R_M)], [rr(R_NM)])
            ts('dve', rows[:, R_IW, sl], rows[:, R_M, sl], -1.0, rsm[:, 1 + c:2 + c], ALU.mult, ALU.add,
               [rr(R_M), 'rsm'], [rr(R_IW)])
            tt('dve', rows[:, R_E1, sl], rows[:, R_NM, sl], rows[:, R_B, sl], ALU.subtract, [rr(R_NM), rr(R_B)],
               [rr(R_E1)])
        n = 64 * nchunks
        act(rows[:, R_IW, 0:n], rows[:, R_IW, 0:n], AF.Exp, [rr(R_IW)], [rr(R_IW)])
        act(rows[:, R_E1, 0:n], rows[:, R_E1, 0:n], AF.Exp, [rr(R_E1)], [rr(R_E1)])
        for k, ri in enumerate([R_A, R_E1, R_IW]):
            mm(psY[0:n, 256 + 4 * k:260 + 4 * k], rows[:, ri, 0:n], identf[0:4, 0:4], True, True, [rr(ri), 'identf'],
               ['psY'])
        cp('dve', cols[0:n, :], psY[0:n, 256:268], ['psY'], ['cols%d' % par])
        cp('dve', rowsB[:, 0, :], rows[:, R_NM, :], [rr(R_NM)], ['rowsB%d' % par])
        cp('dve', rowsB[:, 1, :], rows[:, R_IW, :], [rr(R_IW)], ['rowsB%d' % par])

    def mlstm_tile(nchunks, lastcol, full, par=0, vpar=None):
        n = 64 * nchunks
        lcs = slice(lastcol, lastcol + 1)
        if vpar is None:
            vpar = par
        cols, rowsB, ktm, v1 = colss[par], rowsBs[par], ktms[par], v1s[vpar]
        rcols, rrows, rktm, rv1 = 'cols%d' % par, 'rowsB%d' % par, 'ktm%d' % par, 'v1_%d' % vpar
        nmv = psS[0:n, 256:512].rearrange("p (h t) -> p h t", h=4)
        for c in range(nchunks):
            sl = slice(64 * c, 64 * c + 64)
            lc = 64 * c + lastcol
            for h in range(4):
                if full:
                    mm(psS[sl, h * 64:(h + 1) * 64], mqkT[:, 4 + h, sl], mqkT[:, h, sl], True, True, ['mqkT'], ['psS0'])
                    mm(psS[sl, 256 + h * 64:256 + (h + 1) * 64], sel_t[:, h, 0:64], rowsB[:, 0, sl], True, True,
                       ['sel', rrows], ['psS0'])
                else:
                    mm(psS[sl, 256 + h * 64 + lastcol:256 + h * 64 + lastcol + 1], sel_t[:, h, 0:64],
                       rowsB[:, 0, lc:lc + 1], True, True, ['sel', rrows], ['psS0'])
        if full:
            tt('dve', tmpW[0:n, :, :], nmv, fap(cols[0:n, 0:4], [[1, 4], [0, 64]]), ALU.add, ['psS0', rcols], ['tmpW'])
            tt('dve', tmpW[0:n, :, :], tmpW[0:n, :, :], fap(cmask_t[0:n, :], [[0, 4], [1, 64]]), ALU.add,
               ['tmpW', 'cmask'], ['tmpW'])
            act(WT[0:n, :, :], tmpW[0:n, :, :], AF.Exp, ['tmpW'], ['WT'])
            tt('dve', sTt[0:n, :, :], psS[0:n, 0:256].rearrange("p (h t) -> p h t", h=4), WT[0:n, :, :], ALU.mult,
               ['psS0', 'WT'], ['sT'])
        else:
            tt('dve', tmpW[0:n, :, lcs], nmv[:, :, lcs], fap(cols[0:n, 0:4], [[1, 4], [0, 1]]), ALU.add,
               ['psS0', rcols], ['tmpW'])
            tt('dve', tmpW[0:n, :, lcs], tmpW[0:n, :, lcs], fap(cmask_t[0:n, lcs], [[0, 4], [1, 1]]), ALU.add,
               ['tmpW', 'cmask'], ['tmpW'])
            act(WT[0:n, :, lcs], tmpW[0:n, :, lcs], AF.Exp, ['tmpW'], ['WT'])
        tt('dve', kwt[0:n, :, :], ktm[0:n, :].rearrange("p (h d) -> p h d", h=4),
           fap(WT[0:n, :, lcs], [[64, 4], [0, 128]]), ALU.mult, [rktm, 'WT'], ['kwt'])
        for c in range(nchunks):
            sl = slice(64 * c, 64 * c + 64)
            lc = 64 * c + lastcol
            if full:
                for h in range(4):
                    mm(hv(psN)[sl, h, 0:129], mqkT[:, h, sl], Cb[:, h, 0:129], True, True, ['mqkT', 'Cb'], ['psN'])
                    mm(hv(psAB)[sl, h, 0:129], sTt[sl, h, :], v1[sl, h, 0:129], True, True, ['sT', rv1], ['psA', 'psB'])
            for h in range(4):
                mm(psS[:, 512 + h * 128:512 + (h + 1) * 128], kwt[sl, h, :], v1[sl, h, 0:128], True, True,
                   ['kwt', rv1], ['psS1'])
                mm(psY[:, 300 + h:301 + h], kwt[sl, h, :], v1[sl, h, 128:129], True, True, ['kwt', rv1], ['psY'])
            ts('dve', dg[:, :], identf[0:4, 0:4], rowsB[:, 1, lc:lc + 1], None, ALU.mult, None, ['identf', rrows],
               ['dg'])
            mm(psY[:, 320:324], ones4[:, :], dg[:, :], True, True, ['ones4', 'dg'], ['psY'])
            cp('dve', sml[:, 0:4], psY[:, 320:324], ['psY'], ['decay'])
            tt('dve', Ctmp[:, :, 0:129], Cst[:, :, 0:129], fap(sml[:, 0:4], [[1, 4], [0, 129]]), ALU.mult,
               ['Cst', 'decay'], ['Ctmp'])
            tt('dve', Cst[:, :, 0:128], Ctmp[:, :, 0:128], psS[:, 512:1024].rearrange("p (h d) -> p h d", h=4), ALU.add,
               ['Ctmp', 'psS1'], ['Cst'])
            tt('dve', Cst[:, :, 128:129], Ctmp[:, :, 128:129], psY[:, 300:304].rearrange("p (h o) -> p h o", o=1),
               ALU.add, ['Ctmp', 'psY'], ['Cst'])
            cp('act', Cb[:, :, 0:129], Cst[:, :, 0:129], ['Cst'], ['Cb'])
        if not full:
            return
        for h in range(4):
            act(qCs[0:n, h, 0:129], hv(psN)[0:n, h, 0:129], AF.Copy, ['psN', rcols], ['qCs'], scale=cols[0:n, 8 + h:9 + h])
        tt('dve', nd[0:n, :, 0:129], qCs[0:n, :, 0:129], hv(psAB)[0:n, :, 0:129], ALU.add, ['qCs', 'psA', 'psB'], ['nd'])
        stt('dve', sml[0:n, 4:8], nd[0:n, :, 128], -1.0, nd[0:n, :, 128], ALU.mult, ALU.max, ['nd'], ['den'])
        tt('dve', sml[0:n, 4:8], sml[0:n, 4:8], cols[0:n, 4:8], ALU.max, ['den', rcols], ['den'])
        recip(sml[0:n, 4:8], ['den'], ['den'])
        tt('dve', hm[0:n, :, :], nd[0:n, :, 0:128], fap(sml[0:n, 4:8], [[1, 4], [0, 128]]), ALU.mult, ['nd', 'den'], ['hm'])
        tt('pool', hm[0:n, :, :], hm[0:n, :, :], ogs[0:n, :].rearrange("p (h d) -> p h d", h=4), ALU.mult,
           ['hm', 'ogs'], ['hm'])
        hsq = tmk[:, 512:1024].rearrange("p (h d) -> p h d", h=4)
        tt('pool', hsq[0:n, :, :], hm[0:n, :, :], hm[0:n, :, :], ALU.mult, ['hm'], ['tmk512'])
        reduce(sml[0:n, 8:12], hsq[0:n, :, :], ALU.add, ['tmk512'], ['hss'])
        ts('dve', sml[0:n, 8:12], sml[0:n, 8:12], 1.0 / 128, EPS, ALU.mult, ALU.add, ['hss'], ['hss'])
        tt('pool', sml[0:n, 8:12], sml[0:n, 8:12], chalf[0:n, 0:4], ALU.pow, ['hss', 'chalf'], ['hss'])
        tt('dve', hm[0:n, :, :], hm[0:n, :, :], fap(sml[0:n, 8:12], [[1, 4], [0, 128]]), ALU.mult, ['hm', 'hss'], ['hm'])
        tt('dve', aout[0:n, 512:1024].rearrange("p (h d) -> p h d", h=4), hm[0:n, :, :],
           normm_t[0:n, :].rearrange("p (h d) -> p h d", h=4), ALU.mult,
           ['hm', 'normm'], ['aout_m'])

    psAB_bf = psAB.bitcast(BF16)

    def attention_tile(slots, vm, halo_cols):
        def bufs(h):
            p = h % 2
            SC = (psS, psN)[p]
            rsc = (['psS0', 'psS1'], ['psN'])[p]
            pst = (psT, psAB_bf)[p]
            rpst = (['psT'], ['psA', 'psB'])[p]
            return p, SC, rsc, pst, rpst

        def scores(h):
            p, SC, rsc, pst, rpst = bufs(h)
            po = 64 * (h % 2)
            blk = h // 2
            mm(SC[:, 0:512], identb[:, :], vmask_b[:, vm, 0:512], True, False, ['identb', 'vmaskb'], rsc)
            mm(SC[:, 512:640], identb[:, :], vmask_b[:, vm, 512:640], True, False, ['identb', 'vmaskb'], rsc)
            for j, s_ in enumerate(slots):
                mm(SC[:, j * 128:(j + 1) * 128], qT[po:po + 64, blk, :], KT[po:po + 64, blk, s_ * 128:(s_ + 1) * 128],
                   False, True, ['qT', 'KT'], rsc)

        def softmax(h):
            p, SC, rsc, pst, rpst = bufs(h)
            S, Pm = S_sbs[p], P_sbs[p]
            c0 = 16 + 4 * p
            tt('dve', S[:, :], SC[:, 0:640], bias_t[:, h, :], ALU.add, rsc + A, ['S_sb%d' % p])
            if halo_cols > 0:
                ts('dve', S[:, 0:halo_cols], S[:, 0:halo_cols], halom_t[:, 0:1], None, ALU.add, None,
                   ['S_sb%d' % p] + A, ['S_sb%d' % p])
            reduce(sml[:, c0:c0 + 1], S[:, :], ALU.max, ['S_sb%d' % p], ['amax%d' % p])
            ts('dve', sml[:, c0:c0 + 1], sml[:, c0:c0 + 1], -1.0, None, ALU.mult, None, ['amax%d' % p], ['amax%d' % p])
            memset('pool', sml[:, c0 + 1:c0 + 2], 0.0, ['asum%d' % p])
            act(Pm[:, :], S[:, :], AF.Exp, ['S_sb%d' % p, 'amax%d' % p, 'asum%d' % p], ['P_sb%d' % p, 'asum%d' % p],
                bias=sml[:, c0:c0 + 1], accum=sml[:, c0 + 1:c0 + 2])

        def transpose_pv(h):
            p, SC, rsc, pst, rpst = bufs(h)
            Pm, PT = P_sbs[p], PT_sbs[p]
            for j in range(5):
                tr(pst[:, j * 128:(j + 1) * 128], Pm[:, j * 128:(j + 1) * 128], identb[:, :], ['P_sb%d' % p, 'identb'], rpst)
            cp('act', PT[:, :], pst[:, 0:640], rpst, ['PT_sb%d' % p])
            for j, s_ in enumerate(slots):
                mm(SC[:, 640:704], PT[:, j * 128:(j + 1) * 128], VR[:, s_, h * 64:(h + 1) * 64], j == 0, j == 4,
                   ['PT_sb%d' % p, 'VR'], rsc)

        def finish(h):
            p, SC, rsc, pst, rpst = bufs(h)
            c0 = 16 + 4 * p
            recip(sml[:, c0 + 1:c0 + 2], ['asum%d' % p], ['asum%d' % p])
            ts('dve', aout[:, h * 64:(h + 1) * 64], SC[:, 640:704], sml[:, c0 + 1:c0 + 2], None, ALU.mult, None,
               rsc + ['asum%d' % p], ['aout_a'])

        scores(0)
        softmax(0)
        for h in range(8):
            if h + 1 < 8:
                scores(h + 1)
            transpose_pv(h)
            if h + 1 < 8:
                softmax(h + 1)
            finish(h)

    def mixer_tile(k, ti, mode, slot, sidx=None, par=0, part='all'):
        ktm, v1 = ktms[par], v1s[par]
        xt = xts[k]
        xres = 'xt%d' % k
        rmsnorm_to_hT(xt, xres)
        is_full = mode in ('full', 'sample')
        nchunks = 1 if mode == 'sample' else 2
        lastcol = 31 if mode == 'sample' else 63
        if mode in ('halo', 'full', 'sample'):
            def ev_k(b0, nb):
                cp('act', KT[:, b0:b0 + nb, slot * 128:(slot + 1) * 128],
                   psA[:, 0:nb * 128].rearrange("p (b t) -> p b t", b=nb), ['psA'], ['KT'])
            project_fm(OFF_AK, 4, ev_k)
            project_tm(OFF_AV, lambda: cp('act', VR[:, slot, :], psY[:, :], ['psY'], ['VR']))
        if mode == 'halo':
            if slot == 3:
                def ev_h(b0, nb):
                    cp('act', uT[:, b0:b0 + nb, 3:131], psA[:, 0:nb * 128].rearrange("p (b t) -> p b t", b=nb),
                       ['psA'], ['uT'])
                project_fm(OFF_MQK, 8, ev_h)
                cp('pool', uT[:, :, 0:3], uT[:, :, 128:131], ['uT'], ['uT'])
            return
        if mode == 'lite':
            def ev_u(b0, nb):
                cp('act', uT[:, 4 + b0:4 + b0 + nb, 3:131], psA[:, 0:nb * 128].rearrange("p (b t) -> p b t", b=nb),
                   ['psA'], ['uT'])
            project_fm(OFF_MQK + 512, 4, ev_u)
            conv_silu(4, 8)
        else:
            def ev_q(b0, nb):
                act(qT[:, b0:b0 + nb, :], psA[:, 0:nb * 128].rearrange("p (b t) -> p b t", b=nb), AF.Copy, ['psA'], ['qT'],
                    scale=0.125)
            project_fm(0, 4, ev_q)

            def ev_u(b0, nb):
                cp('act', uT[:, b0:b0 + nb, 3:131], psA[:, 0:nb * 128].rearrange("p (b t) -> p b t", b=nb), ['psA'], ['uT'])
            project_fm(OFF_MQK, 8, ev_u)
            conv_silu(0, 8)
            def ev_og():
                act(ogs[:, :], psY[:, :], AF.Exp, ['psY'], ['ogs'], scale=-1.0)
                act(ogs[:, :], ogs[:, :], AF.Ln, ['ogs'], ['ogs'], bias=1.0)
                act(ogs[:, :], ogs[:, :], AF.Exp, ['ogs'], ['ogs'], scale=-1.0)
            project_tm(OFF_MO, ev_og)
        project_tm(OFF_MV, lambda: cp('act', v1[:, :, 0:128], psY[:, :].rearrange("p (h d) -> p h d", h=4), ['psY'], ['v1_%d' % par]))
        for h in range(4):
            tr(psT[:, h * 128:(h + 1) * 128], mqkT[:, 4 + h, :], identb[:, :], ['mqkT', 'identb'], ['psT'])
        cp('act', ktm[:, :], psT[:, 0:512], ['psT'], ['ktm%d' % par])
        gates_rows(nchunks, lastcol, par)
        if mode == 'full':
            pi = ti - NPRE
            attention_tile([(slot + 1 + j) % 5 for j in range(5)], 0, max(0, 512 - 128 * pi))
        elif mode == 'sample':
            attention_tile([0, 1, 2, 3, 4], 1, 0)
        if part == 'front':
            return
        mlstm_tile(nchunks, lastcol, is_full, par)
        if mode == 'lite':
            return
        if mode == 'sample' or ti - NPRE >= NP - 4:
            ev = lambda: cp('act', tmk[:, 0:512], psY[:, :], ['psY'], ['tmk0'])
            for (col, o_s, o_p, nm) in [(OFF_AK, ks_o, kp_o, 'k'), (OFF_AV, vs_o, vp_o, 'v')]:
                project_tm(col, ev)
                if mode == 'sample':
                    out_ops.append(dma('sp', o_s[sidx * 32:(sidx + 1) * 32, :], tmk[0:32, 0:512], ['tmk0'],
                                       ['o_s' + nm], 'o_tmk'))
                else:
                    r0 = (ti - NPRE - (NP - 4)) * 128
                    out_ops.append(dma('sp', o_p[r0:r0 + 128, :], tmk[:, 0:512], ['tmk0'], ['o_p' + nm], 'o_tmk'))
        if mode == 'sample' or ti - NPRE == NP - 1:
            for half in range(2):
                project_tm(OFF_MQK + 512 * half, lambda: cp('act', tmk[:, 0:512], psY[:, :], ['psY'], ['tmk0']))
                if mode == 'sample':
                    out_ops.append(dma('sp', convs_o[sidx, :, half * 512:(half + 1) * 512], tmk[29:32, 0:512], ['tmk0'],
                                       ['o_cs'], 'o_tmk'))
                else:
                    out_ops.append(dma('sp', convp_o[:, half * 512:(half + 1) * 512], tmk[125:128, 0:512], ['tmk0'],
                                       ['o_cp'], 'o_tmk'))
        for c in range(8):
            tr(psT[:, c * 128:(c + 1) * 128], aout[:, c * 128:(c + 1) * 128], identb[:, :], ['aout_a', 'aout_m', 'identb'],
               ['psT'])
        cp('act', mixT[:, :, :], psT[:, :].rearrange("p (k t) -> p k t", k=8), ['psT'], ['mixT'])
        for half in range(2):
            for kc in range(8):
                mm(psY[:, :], mixT[:, kc, :], woutv[:, kc, half * 512:(half + 1) * 512], kc == 0, kc == 7,
                   A + ['mixT'], ['psY'])
            tt('dve', xt[:, half * 512:(half + 1) * 512], psY[:, :], xt[:, half * 512:(half + 1) * 512], ALU.add,
               ['psY', xres], [xres])
        oi = ti - NPRE
        dma('sp', x2_d[oi * 128:(oi + 1) * 128, :], xt[:, :], [xres], ['x2_%d' % oi], xres + 's')

    def mload(ti):
        return lambda: xload(x1_d[ti * 128:(ti + 1) * 128, :], 'x1_%d' % ti)

    def reset_state(q):
        ts('dve', Cst[:, :, 0:129], Cst[:, :, 0:129], flg_t[:, q:q + 1], None, ALU.mult, None, ['Cst', 'wA'], ['Cst'])
        cp('act', Cb[:, :, 0:129], Cst[:, :, 0:129], ['Cst'], ['Cb'])
        ts('dve', rsm[:, 8:9], rsm[:, 8:9], flg_t[0:4, q:q + 1], None, ALU.mult, None, ['m_in', 'wA'], ['m_in'])
        ts('dve', uT[:, :, 0:3], uT[:, :, 0:3], flg_t[:, q:q + 1], None, ALU.mult, None, ['uT', 'wA'], ['uT'])

    def sample_tiles():
        for s in range(4):
            ti = NPRE + NP + s
            dma('pool', ckl[:, :, :], ck_d[s].rearrange("(kt p) f -> p kt f", p=128), (), ['ckl'] + A, 'ckl')
            dma('pool', VR[:, 0:4, :], cv_d[s].rearrange("(kt p) f -> p kt f", p=128), (), ['VR'] + A, 'cvl')
            for kt in range(4):
                for b in range(4):
                    tr(psT[:, b * 128:(b + 1) * 128], ckl[:, kt, b * 128:(b + 1) * 128], identb[:, :], ['ckl', 'identb'],
                       ['psT'])
                cp('act', KT[:, :, kt * 128:(kt + 1) * 128], psT[:, 0:512].rearrange("p (b t) -> p b t", b=4), ['psT'],
                   ['KT'])
            dma('sp', Cst[:, :, 0:128], sC_d[s].rearrange("h p d -> p h d"), (), ['Cst'] + A, 'sCl')
            dma('sp', Cst[:, :, 128:129], bass.AP(sn_d, s * 512, [[1, 128], [128, 4], [1, 1]]), (), ['Cst'] + A, 'snl',
                slow=True)
            dma('sp', rsm[:, 8:9], bass.AP(sm_d, s * 4, [[1, 4], [1, 1]]), (), ['m_in'] + A, 'sml', slow=True)
            cp('act', Cb[:, :, 0:129], Cst[:, :, 0:129], ['Cst'], ['Cb'])
            for r_ in range(3):
                dma('sp', uT[:, :, r_], bass.AP(sconv_d, (s * 3 + r_) * D, [[1, 128], [128, 8]]), (), ['uT'] + A,
                    'scl%d' % r_, slow=True)
            xload(x1_d[ti * 128:(ti + 1) * 128, :], 'x1_%d' % ti)
            mixer_tile(pend.pop(0), ti, 'sample', 4, sidx=s)
            out_ops.append(dma('sp', Cs_o[s].rearrange("h p d -> p h d"), Cst[:, :, 0:128], ['Cst'], ['o_Cs'], 'o_C'))
            out_ops.append(dma('sp', bass.AP(ns_o, s * 512, [[1, 128], [128, 4], [1, 1]]), Cst[:, :, 128:129], ['Cst'],
                               ['o_ns'], 'o_n', slow=True))
            out_ops.append(dma('sp', bass.AP(ms_o, s * 4, [[1, 4], [1, 1]]), rsm[:, 8:9], ['m_in'], ['o_ms'], 'o_m',
                               slow=True))

    jobsM = []
    if MODE_X:
        memset('pool', rsm[:, 8:9], -1.0e30, ['m_in'], ['m_in'])
        jobsB = [(mload(3), lambda k: mixer_tile(k, 3, 'halo', 3))]
        for pi in range(NP):
            ti = NPRE + pi
            jobsB.append((mload(ti), lambda k, ti=ti: mixer_tile(k, ti, 'lite', 0)))
        run_jobs(jobsB)
        dma('sp', sum_in[0:512, :].rearrange("(h p) c -> p h c", p=128), Cst[:, :, :], ['Cst'], ['dram_sum'], 'sumC')
        dma('sp', sum_in[512:516, 0:2], rsm[:, 8:10], ['m_in'], ['dram_sum'], 'sumM')
        P.op('pool', lambda e: e.collective_compute("AllGather", ALU.bypass, replica_groups=[list(range(NCORES))],
                                                    ins=[sum_in.ap().opt()], outs=[sum_all.ap().opt()]),
             r=['dram_sum'], w=['dram_all'], cc=True)
        sample_tiles()
        for r_ in range(NCORES):
            dma('sp', scal[:, r_, :, :], bass.AP(sum_all, (r_ * SR + 512) * 132, [[0, 128], [132, 4], [1, 2]]),
                ['dram_all'], ['scal'] + A, 'scal')
        memset('pool', Cst[:, :, :], 0.0, ['Cst'], ['Cst'])
        memset('pool', fold[:, 0:4], 0.0, ['fold_m'], A)
        F = lambda i: fold[:, 4 * i:4 * i + 4]
        for r_ in range(NCORES):
            dma('sp', Csum[:, :, :], sum_all[r_ * SR:r_ * SR + 512, :].rearrange("(h p) c -> p h c", p=128),
                ['dram_all'], ['Csum'] + A, 'Csum')
            e_ = fle_t[:, r_:r_ + 1]
            ts('dve', F(1), scal[:, r_, :, 1], e_, None, ALU.mult, None, ['scal'] + A, ['f1'])
            ts('dve', F(2), scal[:, r_, :, 0], e_, flm_t[:, r_:r_ + 1], ALU.mult, ALU.add, ['scal'] + A, ['f2'])
            tt('dve', F(1), F(1), F(0), ALU.add, ['f1', 'fold_m'], ['f1'])
            tt('dve', F(3), F(1), F(2), ALU.max, ['f1', 'f2'], ['f3'])
            tt('dve', F(1), F(1), F(3), ALU.subtract, ['f1', 'f3'], ['f1'])
            tt('dve', F(2), F(2), F(3), ALU.subtract, ['f2', 'f3'], ['f2'])
            act(F(1), F(1), AF.Exp, ['f1'], ['f1'])
            act(F(2), F(2), AF.Exp, ['f2'], ['f2'])
            tt('dve', Cst[:, :, :], Cst[:, :, :], fap(fold[:, 4:8], [[1, 4], [0, 132]]), ALU.mult, ['Cst', 'f1'], ['Cst'])
            tt('dve', Csum[:, :, :], Csum[:, :, :], fap(fold[:, 8:12], [[1, 4], [0, 132]]), ALU.mult, ['Csum', 'f2'],
               ['Csum'])
            tt('dve', Cst[:, :, :], Cst[:, :, :], Csum[:, :, :], ALU.add, ['Cst', 'Csum'], ['Cst'])
            cp('dve', F(0), F(3), ['f3'], ['fold_m'])
        cp('act', Cb[:, :, 0:129], Cst[:, :, 0:129], ['Cst'], ['Cb'])
        tt('dve', fold[0:4, 16:20], fold[0:4, 0:4], identf[0:4, 0:4], ALU.mult, ['fold_m', 'identf'], ['f4'])
        reduce(rsm[:, 8:9], fold[0:4, 16:20], ALU.add, ['f4'], ['m_in'])
    else:
        lite = [(q, pi) for q in range(3) for pi in range(NP)]

        uviews = [uT[:, 4:8, :], uTb[:, :, :]]

        def lF1(i):
            q, pi = lite[i]
            ti = q * NP + pi
            par = i % 2
            uv, up = uviews[par], uviews[1 - par]
            ru, rup = 'uTk%d' % par, 'uTk%d' % (1 - par)
            xload(x1_d[ti * 128:(ti + 1) * 128, :], 'x1_%d' % ti)
            k = pend.pop(0)
            rmsnorm_to_hT(xts[k], 'xt%d' % k)

            def ev_u(b0, nb):
                cp('act', uv[:, b0:b0 + nb, 3:131], psA[:, 0:nb * 128].rearrange("p (b t) -> p b t", b=nb), ['psA'], [ru])
            project_fm(OFF_MQK + 512, 4, ev_u)
            cp('pool', uv[:, :, 0:3], up[:, :, 128:131], [rup], [ru])
            if pi == 0 and q > 0:
                ts('dve', uv[:, :, 0:3], uv[:, :, 0:3], flg_t[:, q - 1:q], None, ALU.mult, None, [ru] + A, [ru])
            v1 = v1s[i % 3]
            project_tm(OFF_MV, lambda: cp('act', v1[:, :, 0:128], psY[:, :].rearrange("p (h d) -> p h d", h=4), ['psY'],
                                          ['v1_%d' % (i % 3)]))
            gates_rows(2, 63, par, part='A')

        def lF2(i):
            q, pi = lite[i]
            par = i % 2
            conv_part('dve', 4, 8, 512, uv=uviews[par], ru='uTk%d' % par)
            ts('dve', mqkT[:, 4:8, :], mqkT[:, 4:8, :], 128 ** -0.5, None, ALU.mult, None, ['mqkT'], ['mqkT'])
            for h in range(4):
                tr(psT[:, h * 128:(h + 1) * 128], mqkT[:, 4 + h, :], identb[:, :], ['mqkT', 'identb'], ['psT'])
            cp('act', ktms[par][:, :], psT[:, 0:512], ['psT'], ['ktm%d' % par])
            gates_rows(2, 63, par, part='B')
            if pi == NP - 1:
                ts('dve', rsm[:, 8:9], rsm[:, 8:9], flg_t[0:4, q:q + 1], None, ALU.mult, None, ['m_in'] + A, ['m_in'])

        def lback(i):
            q, pi = lite[i]
            mlstm_tile(2, 63, False, i % 2, vpar=i % 3)
            if pi == NP - 1:
                ts('dve', Cst[:, :, 0:129], Cst[:, :, 0:129], flg_t[:, q:q + 1], None, ALU.mult, None, ['Cst'] + A,
                   ['Cst'])
                cp('act', Cb[:, :, 0:129], Cst[:, :, 0:129], ['Cst'], ['Cb'])
        nl = len(lite)
        lF1(0)
        lF1(1)
        lF2(0)
        for i in range(nl):
            if i + 2 < nl:
                lF1(i + 2)
            if i + 1 < nl:
                lF2(i + 1)
            lback(i)
    for j in range(4):
        ti = NPRE - 4 + j
        jobsM.append((mload(ti), lambda k, ti=ti, j=j: mixer_tile(k, ti, 'halo', j)))
    for pi in range(NP):
        ti = NPRE + pi
        jobsM.append((mload(ti), lambda k, ti=ti, pi=pi: mixer_tile(k, ti, 'full', (4 + pi) % 5)))
    run_jobs(jobsM)
    out_ops.append(dma('sp', Cp_o[:, :, :].rearrange("h p d -> p h d"), Cst[:, :, 0:128], ['Cst'], ['o_Cp'], 'o_C'))
    out_ops.append(dma('sp', bass.AP(np_o, 0, [[1, 128], [128, 4], [1, 1]]), Cst[:, :, 128:129], ['Cst'], ['o_np'], 'o_n',
                       slow=True))
    out_ops.append(dma('sp', mp_o[:, :], rsm[:, 8:9], ['m_in'], ['o_mp'], 'o_m', slow=True))
    if not MODE_X:
        sample_tiles()

    P.barrier()
    load_gain(2)
    load_ffn_weights("w1b", "w3b", "w2b")
    ffn_phase([(x2_d[i * 128:(i + 1) * 128, :], 'x2_%d' % i, y_o[i * 128:(i + 1) * 128, :], 'o_y')
               for i in range(NOUT)], True, out_ops)

    nsem, maxcnt = P.emit(out_ops)
    info = dict(nops=len(P.ops), nsem=nsem, maxcnt=maxcnt, sbuf_used=sbuf_used)
    return nc, info


MODE_X = False


def _prep_inputs(inp, NP):
    S = 4 * NP * 128
    xp = np.asarray(inp['x_prompt'], np.float32)
    xs = np.asarray(inp['x_sample'], np.float32)
    NPRE = 4 if MODE_X else 3 * NP
    TT = NPRE + NP + 4
    ii = np.arange(128)[:, None]
    jj = np.arange(640)[None, :]
    idx = np.clip(ii + 512 - jj, -128, 128) + 128
    rb = np.asarray(inp['rel_bias'], np.float32)[0]
    relexp = np.ascontiguousarray(np.transpose(rb[:, idx], (1, 0, 2))).reshape(128, 8 * 640)
    vmask = np.full((2, 128, 640), MASKV, np.float32)
    vmask[0][(ii < 64) & (jj < 576) | (ii >= 64) & (jj >= 64)] = 0.0
    vmask[1][:, :544] = 0.0
    cmask = np.where((np.arange(128)[:, None] % 64) <= np.arange(64)[None, :], 0.0, MASKV).astype(np.float32)
    ident = np.eye(128, dtype=np.float32)
    sel = np.zeros((4, 4, 128), np.float32)
    for h in range(4):
        sel[h, h, :] = 1.0
    cwv = np.asarray(inp['conv_w'], np.float32)[0]
    cw = np.ascontiguousarray(cwv.reshape(4, 8, 128).transpose(2, 1, 0)).reshape(128, 32)
    cb = np.ascontiguousarray(np.asarray(inp['conv_b'], np.float32)[0].reshape(8, 128).T)
    gb = np.ascontiguousarray(np.asarray(inp['gate_bias'], np.float32)[0].reshape(2, 4).T)
    gains = np.stack([np.asarray(inp['norm_ffn1'], np.float32)[0], np.asarray(inp['norm_mix'], np.float32)[0],
                      np.asarray(inp['norm_ffn2'], np.float32)[0], np.asarray(inp['norm_final'], np.float32)])
    shared = dict(
        w1a=np.asarray(inp['w1_ffn1'], np.float32)[0], w3a=np.asarray(inp['w3_ffn1'], np.float32)[0],
        w2a=np.asarray(inp['w2_ffn1'], np.float32)[0], w1b=np.asarray(inp['w1_ffn2'], np.float32)[0],
        w3b=np.asarray(inp['w3_ffn2'], np.float32)[0], w2b=np.asarray(inp['w2_ffn2'], np.float32)[0],
        w_in=np.asarray(inp['w_in'], np.float32)[0], w_out=np.asarray(inp['w_out'], np.float32)[0],
        gains=gains, normm=np.asarray(inp['norm_mlstm_out'], np.float32), cw=cw, cb=cb, gb=gb,
        relexp=relexp, vmask=vmask, cmask=cmask, ident=ident, sel=sel.reshape(4, 512))
    ck = np.asarray(inp['cache_attn_k'], np.float32)[0].reshape(32, 512, 512)
    cv = np.asarray(inp['cache_attn_v'], np.float32)[0].reshape(32, 512, 512)
    sC = np.asarray(inp['state_mlstm_C'], np.float32)[0]
    sn = np.asarray(inp['state_mlstm_n'], np.float32)[0]
    sm = np.asarray(inp['state_mlstm_m'], np.float32)[0]
    sconv = np.asarray(inp['state_mlstm_conv'], np.float32)[0]
    maps = []
    L = NP * 128
    for c in range(NCORES):
        b, seg = c // 4, c % 4
        xin = np.zeros((TT * 128, D), np.float32)
        flg = np.zeros((128, 4), np.float32)
        fle = np.zeros((128, 8), np.float32)
        for r_ in range(NCORES):
            if r_ // 4 == b and r_ % 4 < seg:
                fle[:, r_] = 1.0
        flm = ((1.0 - fle) * -1.0e30).astype(np.float32)
        if MODE_X:
            if seg > 0:
                xin[0:512] = xp[b, seg * L - 512:seg * L]
        else:
            for q in range(3):
                src_seg = seg - 3 + q
                if src_seg >= 0:
                    xin[q * L:(q + 1) * L] = xp[b, src_seg * L:(src_seg + 1) * L]
                    flg[:, q] = 1.0
        xin[NPRE * 128:NPRE * 128 + L] = xp[b, seg * L:(seg + 1) * L]
        for s in range(4):
            r0 = (NPRE + NP + s) * 128
            xin[r0:r0 + 32] = xs[4 * c + s]
        m = dict(shared)
        m.update(xin=xin, flg=flg, fle=fle, flm=flm, halom=np.full((128, 1), MASKV if seg == 0 else 0.0, np.float32),
                 ck=np.ascontiguousarray(ck[4 * c:4 * c + 4]), cv=np.ascontiguousarray(cv[4 * c:4 * c + 4]),
                 sC=np.ascontiguousarray(sC[4 * c:4 * c + 4]), sn=np.ascontiguousarray(sn[4 * c:4 * c + 4]),
                 sm=np.ascontiguousarray(sm[4 * c:4 * c + 4]), sconv=np.ascontiguousarray(sconv[4 * c:4 * c + 4]))
        maps.append(m)
    return maps


def _assemble(res, NP):
    L = NP * 128
    S = 4 * L
    f = np.float32
    y_p = np.zeros((2, S, D), f)
    y_s = np.zeros((32, 32, D), f)
    kp = np.zeros((1, 2, 512, 8, 64), f)
    vp = np.zeros((1, 2, 512, 8, 64), f)
    Cp = np.zeros((1, 2, 4, 128, 128), f)
    npp = np.zeros((1, 2, 4, 128), f)
    mp = np.zeros((1, 2, 4), f)
    convp = np.zeros((1, 2, 3, D), f)
    ks = np.zeros((1, 32, 32, 8, 64), f)
    vs = np.zeros((1, 32, 32, 8, 64), f)
    Cs = np.zeros((1, 32, 4, 128, 128), f)
    ns = np.zeros((1, 32, 4, 128), f)
    ms = np.zeros((1, 32, 4), f)
    convs = np.zeros((1, 32, 3, D), f)
    for c in range(NCORES):
        r = res[c]
        b, seg = c // 4, c % 4
        y_p[b, seg * L:(seg + 1) * L] = r['y'][0:L]
        for s in range(4):
            q = 4 * c + s
            y_s[q] = r['y'][(NP + s) * 128:(NP + s) * 128 + 32]
            ks[0, q] = r['ks'][s * 32:(s + 1) * 32].reshape(32, 8, 64)
            vs[0, q] = r['vs'][s * 32:(s + 1) * 32].reshape(32, 8, 64)
            Cs[0, q] = r['Cs'][s]
            ns[0, q] = r['ns'][s]
            ms[0, q] = r['ms'][s]
            convs[0, q] = r['convs'][s]
        if seg == 3:
            kp[0, b] = r['kp'].reshape(512, 8, 64)
            vp[0, b] = r['vp'].reshape(512, 8, 64)
            Cp[0, b] = r['Cp']
            npp[0, b] = r['npo']
            mp[0, b] = r['mpo'][:, 0]
            convp[0, b] = r['convp']
    return (y_p, y_s, kp, vp, Cp, npp, mp, convp, ks, vs, Cs, ns, ms, convs)


def run(inputs, NP):
    nc, info = build(NP)
    maps = _prep_inputs(inputs, NP)
    res = run_bass_kernel_spmd(nc, maps, core_ids=list(range(NCORES)))
    return _assemble(res.results, NP), info


def kernel(**inputs):
    import jax
    jax.devices("cpu")
    out, _ = run(inputs, 32)
    return out
```

```python
import numpy as np
import concourse.bass as bass
import concourse.mybir as mybir
from concourse.bass_utils import run_bass_kernel_spmd

F32 = mybir.dt.float32
BF16 = mybir.dt.bfloat16
AF = mybir.ActivationFunctionType
ALU = mybir.AluOpType
AX = mybir.AxisListType

D = 1024
DFF = 2816
NJ = DFF // 128
DIN = 3592
EPS = 1e-6
MASKV = -30000.0
NCORES = 8
OFF_AK, OFF_AV, OFF_MQK, OFF_MV, OFF_MO, OFF_MG = 512, 1024, 1536, 2560, 3072, 3584


class Prog:
    def __init__(self, nc):
        self.nc = nc
        self.ops = []
        self.last_w = {}
        self.readers = {}

    def op(self, eng, fn, r=(), w=(), dma=None, cc=False):
        i = len(self.ops)
        deps = set()
        for res in r:
            if res in self.last_w:
                deps.add(self.last_w[res])
        for res in w:
            if res in self.last_w:
                deps.add(self.last_w[res])
            for j in self.readers.get(res, ()):
                deps.add(j)
        deps.discard(i)
        self.ops.append(dict(eng=eng, fn=fn, deps=deps, dma=dma, signal=False, sig=None, cc=cc))
        for res in r:
            self.readers.setdefault(res, []).append(i)
        for res in w:
            self.last_w[res] = i
            self.readers[res] = []
        return i

    def barrier(self):
        last = {}
        for idx, o in enumerate(self.ops):
            if o['fn'] is None:
                continue
            key = ('dma', o['dma']) if o['dma'] is not None else ('eng', o['eng'])
            last[key] = idx
        deps = set(last.values())
        for eng in ('sp', 'act', 'dve', 'pool', 'pe'):
            self.ops.append(dict(eng=eng, fn=None, deps=set(deps), dma=None, signal=False, sig=None, cc=False))
        self.last_w = {}
        self.readers = {}

    def emit(self, final_deps):
        nc = self.nc
        ops = self.ops
        ops.append(dict(eng='sp', fn=None, deps=set(final_deps), dma=None, signal=False, sig=None, cc=False))
        for o in ops:
            keep = set()
            for j in o['deps']:
                pj = ops[j]
                if (pj['dma'] is None and o['dma'] is None and pj['eng'] == 'pe' and o['eng'] == 'pe'
                        and o['fn'] is not None):
                    continue
                keep.add(j)
                pj['signal'] = True
            o['deps'] = keep
        cnt = {}
        sems = {}
        for o in ops:
            if not o['signal']:
                continue
            if o['cc']:
                key, inc = ('cc', id(o)), 1
            else:
                key = ('dma', o['dma']) if o['dma'] is not None else ('eng', o['eng'])
                inc = 16 if o['dma'] is not None else 1
            cnt[key] = cnt.get(key, 0) + inc
            o['sig'] = (key, cnt[key], inc)
            if key not in sems:
                sems[key] = nc.alloc_semaphore("s%d" % len(sems))
        per_eng = {}
        for o in ops:
            per_eng.setdefault(o['eng'], []).append(o)

        def run(engname):
            def body(e):
                waited = {}
                for o in per_eng.get(engname, []):
                    need = {}
                    for j in o['deps']:
                        key, val, _ = ops[j]['sig']
                        if val > need.get(key, 0):
                            need[key] = val
                    for key, val in need.items():
                        if waited.get(key, 0) >= val:
                            continue
                        e.wait_ge(sems[key], val)
                        waited[key] = val
                    if o['fn'] is None:
                        continue
                    ins = o['fn'](e)
                    if o['signal']:
                        if o['cc']:
                            ins.then_inc(sems[o['sig'][0]])
                        else:
                            ins.then_inc(sems[o['sig'][0]], o['sig'][2])
            return body

        with nc.Block() as block:
            block.sync(run('sp'))
            block.scalar(run('act'))
            block.vector(run('dve'))
            block.gpsimd(run('pool'))
            block.tensor(run('pe'))
        return len(sems), max(cnt.values()) if cnt else 0


def fap(t, free):
    return bass.AP(t.tensor, t.offset, [list(t.ap[0])] + [list(f) for f in free])


def build(NP):
    nc = bass.Bass("TRN2", target_bir_lowering=False)
    P = Prog(nc)
    NPRE = 4 if MODE_X else 3 * NP
    TT = NPRE + NP + 4
    NOUT = NP + 4
    SR = 4 * 128 + 8

    def din(name, shape):
        return nc.dram_tensor(name, list(shape), F32, kind="ExternalInput")

    def dout(name, shape):
        return nc.dram_tensor(name, list(shape), F32, kind="ExternalOutput")

    xin = din("xin", [TT * 128, D])
    W = {}
    for nm, shp in [("w1a", [D, DFF]), ("w3a", [D, DFF]), ("w2a", [DFF, D]),
                    ("w1b", [D, DFF]), ("w3b", [D, DFF]), ("w2b", [DFF, D]),
                    ("w_in", [D, DIN]), ("w_out", [D, D])]:
        W[nm] = din(nm, shp)
    gains_d = din("gains", [4, D])
    normm_d = din("normm", [1, 512])
    cw_d = din("cw", [128, 8 * 4])
    cb_d = din("cb", [128, 8])
    gb_d = din("gb", [4, 2])
    relexp_d = din("relexp", [128, 8 * 640])
    vmask_d = din("vmask", [2, 128, 640])
    cmask_d = din("cmask", [128, 64])
    ident_d = din("ident", [128, 128])
    sel_d = din("sel", [4, 4 * 128])
    halom_d = din("halom", [128, 1])
    flg_d = din("flg", [128, 4])
    fle_d = din("fle", [128, 8])
    flm_d = din("flm", [128, 8])
    ck_d = din("ck", [4, 512, 512])
    cv_d = din("cv", [4, 512, 512])
    sC_d = din("sC", [4, 4, 128, 128])
    sn_d = din("sn", [4, 4, 128])
    sm_d = din("sm", [4, 4])
    sconv_d = din("sconv", [4, 3, D])

    y_o = dout("y", [NOUT * 128, D])
    kp_o = dout("kp", [512, 512])
    vp_o = dout("vp", [512, 512])
    Cp_o = dout("Cp", [4, 128, 128])
    np_o = dout("npo", [4, 128])
    mp_o = dout("mpo", [4, 1])
    convp_o = dout("convp", [3, D])
    ks_o = dout("ks", [4 * 32, 512])
    vs_o = dout("vs", [4 * 32, 512])
    Cs_o = dout("Cs", [4, 4, 128, 128])
    ns_o = dout("ns", [4, 4, 128])
    ms_o = dout("ms", [4, 4])
    convs_o = dout("convs", [4, 3, D])

    x1_d = nc.dram_tensor("x1s", [TT * 128, D], F32)
    x2_d = nc.dram_tensor("x2s", [NOUT * 128, D], F32)
    sum_in = nc.dram_tensor("sum_in", [SR, 132], F32)
    sum_all = nc.dram_tensor("sum_all", [NCORES * SR, 132], F32)

    SB_END = 229344
    ptr = [16512]

    def sb(name, shape, dt):
        sz = int(np.prod(shape[1:])) * (2 if dt == BF16 else 4)
        sz = (sz + 31) // 32 * 32
        t = nc.alloc_sbuf_tensor_at(name, list(shape), dt, offset=ptr[0])
        ptr[0] += sz
        assert ptr[0] <= SB_END, (name, ptr[0])
        return t

    gain_t = sb("gain", [128, D], F32)
    gainf_t = sb("gainf", [128, D], F32)
    identf = sb("identf", [128, 128], F32)
    identb = sb("identb", [128, 128], BF16)
    st = sb("st", [128, 8], F32)
    chalf = sb("chalf", [128, 8], F32)
    xts = [sb("xt%d" % i, [128, D], F32) for i in range(2)]
    hb = sb("hb", [128, D], BF16)
    junk = sb("junk", [128, D], BF16)
    hTg = sb("hTg", [128, 8, 512], BF16)
    hT = hTg[:, :, 0:128]
    arena0 = ptr[0]
    wA = sb("wA", [128, 8 * DFF], BF16)
    wB = sb("wB", [128, 8 * DFF], BF16)
    wC = sb("wC", [128, NJ * D], BF16)
    xts += [sb("xt%d" % i, [128, D], F32) for i in range(2, 6)]
    gff = sb("gff", [128, NJ, 512], BF16)
    sas = [sb("sa%d" % i, [128, 512], F32) for i in range(2)]
    arena_end = ptr[0]
    w1v = wA[:, :].rearrange("p (k n) -> p k n", k=8)
    w3v = wB[:, :].rearrange("p (k n) -> p k n", k=8)
    w2v = wC[:, :].rearrange("p (j n) -> p j n", j=NJ)
    optr = [arena0]

    def ov(name, shape, dt):
        sz = int(np.prod(shape[1:])) * (2 if dt == BF16 else 4)
        sz = (sz + 31) // 32 * 32
        t = nc.alloc_sbuf_tensor_at(name, list(shape), dt, offset=optr[0])
        optr[0] += sz
        assert optr[0] <= SB_END, (name, optr[0], SB_END)
        return t

    winv = ov("win", [128, 8, DIN], BF16)
    woutv = ov("wout", [128, 8, D], BF16)
    bias_t = ov("bias", [128, 8, 640], F32)
    vmask_b = ov("vmaskb", [128, 2, 640], BF16)
    KT = ov("KT", [128, 4, 640], BF16)
    VR = ov("VR", [128, 5, 512], BF16)
    qT = ov("qT", [128, 4, 128], BF16)
    uT = ov("uT", [128, 8, 131], F32)
    cacc = ov("cacc", [128, 8, 128], F32)
    uTb = ov("uTb", [128, 4, 131], F32)
    rowsGs = [ov("rowsG%d" % i, [4, 2, 128], F32) for i in range(2)]
    tmk = ov("tmk", [128, 1024], F32)
    mqkT = ov("mqkT", [128, 8, 128], BF16)
    ktms = [ov("ktm%d" % i, [128, 512], BF16) for i in range(2)]
    kwt = ov("kwt", [128, 4, 128], BF16)
    v1s = [ov("v1_%d" % i, [128, 4, 132], BF16) for i in range(3)]
    ogs = ov("ogs", [128, 512], F32)
    S_sbs = [ov("S_sb%d" % i, [128, 640], F32) for i in range(2)]
    P_sbs = [ov("P_sb%d" % i, [128, 640], BF16) for i in range(2)]
    PT_sbs = [ov("PT_sb%d" % i, [128, 640], BF16) for i in range(2)]
    aout = ov("aout", [128, 1024], BF16)
    mixT = ov("mixT", [128, 8, 128], BF16)
    Cst = ov("Cst", [128, 4, 132], F32)
    Cb = ov("Cb", [128, 4, 132], BF16)
    Ctmp = ov("Ctmp", [128, 4, 132], F32)
    qCs = ov("qCs", [128, 4, 132], F32)
    nd = ov("nd", [128, 4, 132], F32)
    hm = ov("hm", [128, 4, 128], F32)
    WT = ov("WT", [128, 4, 64], F32)
    sTt = ov("sT", [128, 4, 64], BF16)
    tmpW = ov("tmpW", [128, 4, 64], F32)
    colss = [ov("cols%d" % i, [128, 12], F32) for i in range(2)]
    rowsBs = [ov("rowsB%d" % i, [4, 2, 128], F32) for i in range(2)]
    cw_t = ov("cw_t", [128, 8, 4], F32)
    cb_t = ov("cb_t", [128, 8], F32)
    normm_t = ov("normm_t", [128, 512], F32)
    cmask_t = ov("cmask_t", [128, 64], F32)
    sel_t = ov("sel_t", [4, 4, 128], F32)
    ones4 = ov("ones4", [4, 128], F32)
    ckl = ov("ckl", [128, 4, 512], BF16)
    sml = ov("sml", [128, 32], F32)
    rows = ov("rows", [4, 12, 128], F32)
    rsm = ov("rsm", [4, 16], F32)
    dg = ov("dg", [4, 4], F32)
    halom_t = ov("halom_t", [128, 1], F32)
    flg_t = ov("flg_t", [128, 4], F32)
    fle_t = ov("fle_t", [128, 8], F32)
    flm_t = ov("flm_t", [128, 8], F32)
    if MODE_X:
        Csum = ov("Csum", [128, 4, 132], F32)
        scal = ov("scal", [128, 8, 4, 2], F32)
        fold = ov("fold", [128, 40], F32)
    gb_t = ov("gb_t", [4, 4], F32)
    sbuf_used = (ptr[0], optr[0], arena_end)

    psT = nc.alloc_psum_tensor("psT", [128, 1024], BF16)
    ps7 = nc.alloc_psum_tensor("ps7", [128, 3584], F32)
    psAB = ps7[:, 0:1024]
    psA = ps7[:, 0:512]
    psB = ps7[:, 512:1024]
    psY = ps7[:, 1024:1536]
    psS = ps7[:, 1536:2560]
    psN = ps7[:, 2560:3584]
    fA = [ps7[:, 0:512], ps7[:, 512:1024]]
    fB = [ps7[:, 1024:1536], ps7[:, 1536:2048]]
    fY = [ps7[:, 2048:2560], ps7[:, 2560:3072]]

    def hv(ps):
        return ps.rearrange("p (h c) -> p h c", h=4)

    def dma(q, out, in_, r, w, key, slow=False):
        if slow:
            return P.op(q, lambda e: e.dma_start(out=out, in_=in_, allow_slow_non_contiguous=True), r=r, w=w, dma=key)
        return P.op(q, lambda e: e.dma_start(out=out, in_=in_), r=r, w=w, dma=key)

    def act(out, in_, func, r, w, bias=None, scale=None, accum=None):
        kw_ = {}
        if bias is not None:
            kw_['bias'] = bias
        if scale is not None:
            kw_['scale'] = scale
        if accum is not None:
            kw_['accum_out'] = accum
        return P.op('act', lambda e: e.activation(out=out, in_=in_, func=func, **kw_), r=r, w=w)

    def tt(eng, out, in0, in1, op, r, w):
        return P.op(eng, lambda e: e.tensor_tensor(out=out, in0=in0, in1=in1, op=op), r=r, w=w)

    def ts(eng, out, in0, s1, s2, op0, op1, r, w):
        if op1 is None:
            return P.op(eng, lambda e: e.tensor_scalar(out=out, in0=in0, scalar1=s1, scalar2=None, op0=op0), r=r, w=w)
        return P.op(eng, lambda e: e.tensor_scalar(out=out, in0=in0, scalar1=s1, scalar2=s2, op0=op0, op1=op1), r=r, w=w)

    def stt(eng, out, in0, sc, in1, op0, op1, r, w):
        return P.op(eng, lambda e: e.scalar_tensor_tensor(out=out, in0=in0, scalar=sc, in1=in1, op0=op0, op1=op1), r=r, w=w)

    def mm(out, lhsT, rhs, start, stop, r, w):
        return P.op('pe', lambda e: e.matmul(out, lhsT, rhs, start=start, stop=stop), r=r, w=w)

    def tr(out, in_, ident, r, w):
        return P.op('pe', lambda e: e.transpose(out, in_, ident), r=r, w=w)

    def cp(eng, out, in_, r, w):
        if eng == 'act':
            return P.op('act', lambda e: e.copy(out=out, in_=in_), r=r, w=w)
        return P.op(eng, lambda e: e.tensor_copy(out=out, in_=in_), r=r, w=w)

    def memset(eng, ap, val, w, r=()):
        return P.op(eng, lambda e: e.memset(ap, val), r=r, w=w)

    def recip(ap, r, w):
        return P.op('dve', lambda e: e.reciprocal(out=ap, in_=ap), r=r, w=w)

    def reduce(out, in_, op, r, w):
        return P.op('dve', lambda e: e.tensor_reduce(out=out, in_=in_, axis=AX.X, op=op), r=r, w=w)

    out_ops = []
    memset('pool', chalf[:, 0:4], -0.5, ['chalf'])
    memset('pool', chalf[:, 4:8], -1.0, ['chalf'])

    dma('sp', identf[:, :], ident_d[:, :], (), ['identf'], 'identf')
    dma('pool', identb[:, :], ident_d[:, :], (), ['identb'], 'identb')
    dma('sp', gainf_t[:, :], bass.AP(gains_d, 3 * D, [[0, 128], [1, D]]), (), ['gainf'], 'gainf')

    def load_gain(idx):
        dma('sp', gain_t[:, :], bass.AP(gains_d, idx * D, [[0, 128], [1, D]]), (), ['gain'], 'gain')

    def rmsnorm_to_hT(xt, xres):
        memset('pool', st[:, 0:1], 0.0, ['ss'])
        act(hb[:, :], xt[:, :], AF.Square, [xres, 'ss'], ['ss', 'hb'], accum=st[:, 0:1])
        ts('dve', st[:, 1:2], st[:, 0:1], 1.0 / D, EPS, ALU.mult, ALU.add, ['ss'], ['rs'])
        tt('pool', st[:, 1:2], st[:, 1:2], chalf[:, 0:1], ALU.pow, ['rs', 'chalf'], ['rs'])
        stt('dve', hb[:, :], xt[:, :], st[:, 1:2], gain_t[:, :], ALU.mult, ALU.mult, [xres, 'rs', 'gain'], ['hb'])
        for c in range(8):
            tr(psT[:, c * 128:(c + 1) * 128], hb[:, c * 128:(c + 1) * 128], identb[:, :], ['hb', 'identb'], ['psT'])
        cp('act', hT[:, :, :], psT[:, :].rearrange("p (k t) -> p k t", k=8), ['psT'], ['hT'])

    def load_ffn_weights(n1, n3, n2):
        for half in range(2):
            dma('pool', w1v[:, 4 * half:4 * half + 4, :],
                W[n1][512 * half:512 * half + 512, :].rearrange("(k p) n -> p k n", p=128), (), ['wA%d' % half],
                'wA%d' % half)
            dma('pool', w3v[:, 4 * half:4 * half + 4, :],
                W[n3][512 * half:512 * half + 512, :].rearrange("(k p) n -> p k n", p=128), (), ['wB%d' % half],
                'wB%d' % half)
        for half in range(2):
            dma('pool', w2v[:, 11 * half:11 * half + 11, :],
                W[n2][1408 * half:1408 * half + 1408, :].rearrange("(j p) n -> p j n", p=128), (), ['wC%d' % half],
                'wC%d' % half)

    slotc = [0]
    pend = []
    NSLOT = [2]

    def xload(src, src_res):
        k = slotc[0] % NSLOT[0]
        slotc[0] += 1
        dma('sp', xts[k][:, :], src, [src_res], ['xt%d' % k], 'xt%dl' % k)
        pend.append(k)

    def run_jobs(jobs):
        if not jobs:
            return
        jobs[0][0]()
        for k, (ld, comp) in enumerate(jobs):
            if k + 1 < len(jobs):
                jobs[k + 1][0]()
            comp(pend.pop(0))

    def ffn_norm_pre(k):
        xt = xts[k]
        xres = 'xt%d' % k
        memset('pool', st[:, 0:1], 0.0, ['ss'])
        act(hb[:, :], xt[:, :], AF.Square, [xres, 'ss'], ['ss', 'hb'], accum=st[:, 0:1])
        ts('dve', st[:, 1:2], st[:, 0:1], 1.0 / D, EPS, ALU.mult, ALU.add, ['ss'], ['rs'])
        tt('pool', st[:, 1:2], st[:, 1:2], chalf[:, 0:1], ALU.pow, ['rs', 'chalf'], ['rs'])
        stt('dve', hb[:, :], xt[:, :], st[:, 1:2], gain_t[:, :], ALU.mult, ALU.mult, [xres, 'rs', 'gain'], ['hb'])

    def ffn_norm_post(t):
        for c in range(8):
            tr(psT[:, c * 128:(c + 1) * 128], hb[:, c * 128:(c + 1) * 128], identb[:, :], ['hb', 'identb'], ['psT'])
        cp('act', hTg[:, :, t * 128:(t + 1) * 128], psT[:, :].rearrange("p (k t) -> p k t", k=8), ['psT'], ['hTg%d' % t])

    def ffn_norm(k, t):
        ffn_norm_pre(k)
        ffn_norm_post(t)

    def ffn_ab():
        hres = ['hTg%d' % t for t in range(4)]
        for j in range(NJ):
            pa, pb = fA[j % 2], fB[j % 2]
            ra, rb_ = 'fA%d' % (j % 2), 'fB%d' % (j % 2)
            for kc in range(8):
                mm(pa, w1v[:, kc, j * 128:(j + 1) * 128], hTg[:, kc, :], kc == 0, kc == 7, ['wA%d' % (kc // 4)] + hres, [ra])
            for kc in range(8):
                mm(pb, w3v[:, kc, j * 128:(j + 1) * 128], hTg[:, kc, :], kc == 0, kc == 7, ['wB%d' % (kc // 4)] + hres, [rb_])
            sa_ = sas[j % 2]
            act(sa_[:, :], pa, AF.Silu, [ra], ['sa%d' % (j % 2)])
            tt('dve', gff[:, j, :], sa_[:, :], pb, ALU.mult, ['sa%d' % (j % 2), rb_], ['gff'])

    def ffn_out(k, t, dst, dst_res, final):
        xt = xts[k]
        xres = 'xt%d' % k
        for half in range(2):
            py = fY[half]
            ry = 'fY%d' % half
            for j in range(NJ):
                mm(py, gff[:, j, t * 128:(t + 1) * 128], w2v[:, j, half * 512:(half + 1) * 512], j == 0, j == NJ - 1,
                   ['gff', 'wC%d' % (j // 11)], [ry])
            stt('dve', xt[:, half * 512:(half + 1) * 512], py, 0.5, xt[:, half * 512:(half + 1) * 512],
                ALU.mult, ALU.add, [ry, xres], [xres])
        if final:
            memset('pool', st[:, 2:3], 0.0, ['ss2'])
            act(junk[:, :], xt[:, :], AF.Square, [xres, 'ss2'], ['ss2', 'junk'], accum=st[:, 2:3])
            ts('dve', st[:, 3:4], st[:, 2:3], 1.0 / D, EPS, ALU.mult, ALU.add, ['ss2'], ['rs2'])
            tt('pool', st[:, 3:4], st[:, 3:4], chalf[:, 0:1], ALU.pow, ['rs2', 'chalf'], ['rs2'])
            stt('dve', xt[:, :], xt[:, :], st[:, 3:4], gainf_t[:, :], ALU.mult, ALU.mult, [xres, 'rs2', 'gainf'], [xres])
        return dma('pool', dst, xt[:, :], [xres], [dst_res], xres + 's')

    def ffn_phase(tiles, final, collect):
        assert len(tiles) % 4 == 0
        NSLOT[0] = 6
        slotc[0] = 0
        del pend[:]
        ng = len(tiles) // 4
        slots = {}

        def load1(g, t):
            src, sres, _, _ = tiles[4 * g + t]
            xload(src, sres)
            slots[(g, t)] = pend.pop(0)
        for t in range(4):
            load1(0, t)
        for t in range(4):
            ffn_norm(slots[(0, t)], t)
        for g in range(ng):
            ffn_ab()
            if g + 1 < ng:
                load1(g + 1, 0)
                load1(g + 1, 1)
            for t in range(4):
                _, _, dst, dres = tiles[4 * g + t]
                if g + 1 < ng:
                    ffn_norm_pre(slots[(g + 1, t)])
                o = ffn_out(slots[(g, t)], t, dst, dres, final)
                if collect is not None:
                    collect.append(o)
                if g + 1 < ng:
                    if t < 2:
                        load1(g + 1, t + 2)
                    ffn_norm_post(t)
        NSLOT[0] = 2
        slotc[0] = 0

    load_gain(0)
    load_ffn_weights("w1a", "w3a", "w2a")
    ffn_phase([(xin[i * 128:(i + 1) * 128, :], 'dram_xin', x1_d[i * 128:(i + 1) * 128, :], 'x1_%d' % i)
               for i in range(TT)], False, None)
    P.barrier()

    A = ['wA', 'wB', 'wC']
    load_gain(1)
    for half in range(2):
        dma('pool', winv[:, 4 * half:4 * half + 4, :],
            W["w_in"][512 * half:512 * half + 512, :].rearrange("(k p) n -> p k n", p=128), (), A, 'win%d' % half)
    dma('pool', woutv[:, :, :], W["w_out"][:, :].rearrange("(k p) n -> p k n", p=128), (), A, 'wout')
    dma('sp', bias_t[:, :, :], relexp_d[:, :].rearrange("p (h k) -> p h k", h=8), (), A, 'bias')
    dma('pool', vmask_b[:, :, :], vmask_d[:, :, :].rearrange("v p k -> p v k"), (), ['vmaskb'] + A, 'vmask')
    dma('sp', cw_t[:, :, :], cw_d[:, :].rearrange("p (b j) -> p b j", b=8), (), A, 'cw')
    dma('sp', cb_t[:, :], cb_d[:, :], (), A, 'cb')
    dma('sp', normm_t[:, :], bass.AP(normm_d, 0, [[0, 128], [1, 512]]), (), A, 'normm')
    dma('sp', cmask_t[:, :], cmask_d[:, :], (), A, 'cmask')
    dma('sp', sel_t[:, :, :], sel_d[:, :].rearrange("k (h m) -> k h m", h=4), (), A, 'sel')
    dma('sp', halom_t[:, :], halom_d[:, :], (), A, 'halom')
    dma('sp', flg_t[:, :], flg_d[:, :], (), A, 'flg')
    dma('sp', fle_t[:, :], fle_d[:, :], (), A, 'fle')
    dma('sp', flm_t[:, :], flm_d[:, :], (), A, 'flm')
    dma('sp', gb_t[:, 0:2], gb_d[:, :], (), A, 'gb')
    ts('dve', gb_t[:, 2:3], gb_t[:, 1:2], -1.0, None, ALU.mult, None, A, ['gbn'])
    for v1_ in v1s:
        memset('pool', v1_[:, :, :], 1.0, ['v1_0', 'v1_1', 'v1_2'], A)
    memset('pool', uT[:, :, :], 0.0, ['uT'], A)
    memset('pool', uTb[:, :, :], 0.0, ['uTk1'], A)
    memset('pool', KT[:, :, :], 0.0, ['KT'], A)
    memset('pool', VR[:, :, :], 0.0, ['VR'], A)
    memset('pool', aout[:, :], 0.0, ['aout_a', 'aout_m'], A)
    memset('pool', ones4[:, :], 1.0, ['ones4'], A)
    memset('pool', Cst[:, :, :], 0.0, ['Cst'], A)
    memset('pool', Cb[:, :, :], 0.0, ['Cb'], A)
    memset('pool', rsm[:, :], 0.0, ['rsm', 'm_in'], A)

    def project_fm(col0, nblk, evac):
        b0 = 0
        while b0 < nblk:
            nb = min(4, nblk - b0)
            for bb in range(nb):
                c0 = col0 + (b0 + bb) * 128
                for kc in range(8):
                    mm(psA[:, bb * 128:(bb + 1) * 128], winv[:, kc, c0:c0 + 128], hT[:, kc, :], kc == 0, kc == 7,
                       A + ['hT'], ['psA'])
            evac(b0, nb)
            b0 += nb

    def project_tm(col0, evac):
        for kc in range(8):
            mm(psY[:, :], hT[:, kc, :], winv[:, kc, col0:col0 + 512], kc == 0, kc == 7, A + ['hT'], ['psY'])
        evac()

    def conv_part(eng, b0, b1, tcol, uv=None, ru='uT'):
        nb = b1 - b0
        if uv is None:
            uv = uT[:, b0:b1, :]
        tv = tmk[:, tcol:tcol + nb * 128].rearrange("p (b t) -> p b t", b=nb)
        rt = 'tmk%d' % tcol
        rc = 'cacc%d' % b0
        tt(eng, cacc[:, b0:b1, :], uv[:, :, 0:128], fap(cw_t[:, b0:b1, 0:1], [[4, nb], [0, 128]]), ALU.mult,
           [ru, 'cw'], [rc])
        for j in range(1, 4):
            tt(eng, tv, uv[:, :, j:j + 128], fap(cw_t[:, b0:b1, j:j + 1], [[4, nb], [0, 128]]), ALU.mult,
               [ru, 'cw'], [rt])
            tt(eng, cacc[:, b0:b1, :], cacc[:, b0:b1, :], tv, ALU.add, [rt, rc], [rc])
        tt(eng, cacc[:, b0:b1, :], cacc[:, b0:b1, :], fap(cb_t[:, b0:b1], [[1, nb], [0, 128]]), ALU.add,
           [rc, 'cb'], [rc])
        act(tv, cacc[:, b0:b1, :], AF.Exp, [rc], [rt], scale=-1.0)
        act(tv, tv, AF.Ln, [rt], [rt], bias=1.0)
        act(tv, tv, AF.Exp, [rt], [rt], scale=-1.0)
        tt(eng, mqkT[:, b0:b1, :], cacc[:, b0:b1, :], tv, ALU.mult, [rc, rt], ['mqkT'])

    def conv_silu(b0, b1):
        if b0 == 0:
            conv_part('pool', 0, 4, 0)
            conv_part('dve', 4, 8, 512)
        else:
            conv_part('dve', b0, b1, 512)
        kb0 = max(b0, 4)
        ts('dve', mqkT[:, kb0:b1, :], mqkT[:, kb0:b1, :], 128 ** -0.5, None, ALU.mult, None, ['mqkT'], ['mqkT'])
        cp('pool', uT[:, b0:b1, 0:3], uT[:, b0:b1, 128:131], ['uT'], ['uT'])

    R_IG, R_L, R_B, R_A, R_M, R_NM, R_E1, R_IW, R_T1, R_T2 = range(10)
    rr = lambda k: 'row%d' % k
    R_ = lambda k: rows[:, k, :]
    v3 = lambda k: rows[:, k, :].rearrange("h (c t) -> h c t", c=2)

    def scan(src, op):
        cur = src
        k = 0
        for d in (1, 2, 4, 8, 16, 32):
            o = (R_T1, R_T2)[k % 2]
            k += 1
            cp('dve', v3(o)[:, :, 0:d], v3(cur)[:, :, 0:d], [rr(cur)], [rr(o)])
            tt('dve', v3(o)[:, :, d:64], v3(cur)[:, :, d:64], v3(cur)[:, :, 0:64 - d], op, [rr(cur)], [rr(o)])
            cur = o
        return cur

    def gates_rows(nchunks, lastcol, par=0, part='all'):
        cols = colss[par]
        rowsB = rowsBs[par]
        rowsG = rowsGs[par]
        rG = 'rowsG%d' % par
        if part in ('all', 'A'):
            for half, g0 in enumerate([OFF_MG, OFF_MG + 4]):
                for kc in range(8):
                    mm(psY[0:4, half * 128:(half + 1) * 128], winv[:, kc, g0:g0 + 4], hT[:, kc, :], kc == 0, kc == 7,
                       A + ['hT'], ['psY'])
            act(rowsG[:, 0, :], psY[0:4, 0:128], AF.Identity, ['psY', 'gb'] + A, [rG], bias=gb_t[:, 0:1])
            act(R_(R_T1), psY[0:4, 128:256], AF.Exp, ['psY', 'gbn'], [rr(R_T1)], bias=gb_t[:, 2:3], scale=-1.0)
            act(rowsG[:, 1, :], R_(R_T1), AF.Ln, [rr(R_T1)], [rG], bias=1.0)
            if part == 'A':
                return
        cp('dve', R_(R_L), rowsG[:, 1, :], [rG], [rr(R_L)])
        cs = scan(R_L, ALU.add)
        ts('dve', R_(R_B), R_(cs), -1.0, None, ALU.mult, None, [rr(cs)], [rr(R_B)])
        tt('dve', R_(R_A), rowsG[:, 0, :], R_(cs), ALU.add, [rG, rr(cs)], [rr(R_A)])
        cm = scan(R_A, ALU.max)
        for c in range(nchunks):
            sl = slice(64 * c, 64 * c + 64)
            lc = 64 * c + lastcol
            cp('dve', rsm[:, 1 + c:2 + c], rsm[:, 8:9], ['m_in'], ['rsm'])
            ts('dve', rows[:, R_M, sl], rows[:, cm, sl], rsm[:, 8:9], None, ALU.max, None, [rr(cm), 'm_in'], [rr(R_M)])
            tt('dve', rsm[:, 8:9], rows[:, R_B, lc:lc + 1], rows[:, R_M, lc:lc + 1], ALU.add, [rr(R_B), rr(R_M)],
               ['m_in'])
            tt('dve', rsm[:, 9:10], rsm[:, 9:10], rows[:, R_B, lc:lc + 1], ALU.add, [rr(R_B), 'm_in'], ['m_in'])
            ts('dve', rows[:, R_NM, sl], rows[:, R_M, sl], -1.0, None, ALU.mult, None, [rr(R_M)], [rr(R_NM)])
            ts('dve', rows[:, R_IW, sl], rows[:, R_M, sl], -1.0, rsm[:, 1 + c:2 + c], ALU.mult, ALU.add,
               [rr(R_M), 'rsm'], [rr(R_IW)])
            tt('dve', rows[:, R_E1, sl], rows[:, R_NM, sl], rows[:, R_B, sl], ALU.subtract, [rr(R_NM), rr(R_B)],
               [rr(R_E1)])
        n = 64 * nchunks
        act(rows[:, R_IW, 0:n], rows[:, R_IW, 0:n], AF.Exp, [rr(R_IW)], [rr(R_IW)])
        act(rows[:, R_E1, 0:n], rows[:, R_E1, 0:n], AF.Exp, [rr(R_E1)], [rr(R_E1)])
        for k, ri in enumerate([R_A, R_E1, R_IW]):
            mm(psY[0:n, 256 + 4 * k:260 + 4 * k], rows[:, ri, 0:n], identf[0:4, 0:4], True, True, [rr(ri), 'identf'],
               ['psY'])
        cp('dve', cols[0:n, :], psY[0:n, 256:268], ['psY'], ['cols%d' % par])
        cp('dve', rowsB[:, 0, :], rows[:, R_NM, :], [rr(R_NM)], ['rowsB%d' % par])
        cp('dve', rowsB[:, 1, :], rows[:, R_IW, :], [rr(R_IW)], ['rowsB%d' % par])

    def mlstm_tile(nchunks, lastcol, full, par=0, vpar=None):
        n = 64 * nchunks
        lcs = slice(lastcol, lastcol + 1)
        if vpar is None:
            vpar = par
        cols, rowsB, ktm, v1 = colss[par], rowsBs[par], ktms[par], v1s[vpar]
        rcols, rrows, rktm, rv1 = 'cols%d' % par, 'rowsB%d' % par, 'ktm%d' % par, 'v1_%d' % vpar
        nmv = psS[0:n, 256:512].rearrange("p (h t) -> p h t", h=4)
        for c in range(nchunks):
            sl = slice(64 * c, 64 * c + 64)
            lc = 64 * c + lastcol
            for h in range(4):
                if full:
                    mm(psS[sl, h * 64:(h + 1) * 64], mqkT[:, 4 + h, sl], mqkT[:, h, sl], True, True, ['mqkT'], ['psS0'])
                    mm(psS[sl, 256 + h * 64:256 + (h + 1) * 64], sel_t[:, h, 0:64], rowsB[:, 0, sl], True, True,
                       ['sel', rrows], ['psS0'])
                else:
                    mm(psS[sl, 256 + h * 64 + lastcol:256 + h * 64 + lastcol + 1], sel_t[:, h, 0:64],
                       rowsB[:, 0, lc:lc + 1], True, True, ['sel', rrows], ['psS0'])
        if full:
            tt('dve', tmpW[0:n, :, :], nmv, fap(cols[0:n, 0:4], [[1, 4], [0, 64]]), ALU.add, ['psS0', rcols], ['tmpW'])
            tt('dve', tmpW[0:n, :, :], tmpW[0:n, :, :], fap(cmask_t[0:n, :], [[0, 4], [1, 64]]), ALU.add,
               ['tmpW', 'cmask'], ['tmpW'])
            act(WT[0:n, :, :], tmpW[0:n, :, :], AF.Exp, ['tmpW'], ['WT'])
            tt('dve', sTt[0:n, :, :], psS[0:n, 0:256].rearrange("p (h t) -> p h t", h=4), WT[0:n, :, :], ALU.mult,
               ['psS0', 'WT'], ['sT'])
        else:
            tt('dve', tmpW[0:n, :, lcs], nmv[:, :, lcs], fap(cols[0:n, 0:4], [[1, 4], [0, 1]]), ALU.add,
               ['psS0', rcols], ['tmpW'])
            tt('dve', tmpW[0:n, :, lcs], tmpW[0:n, :, lcs], fap(cmask_t[0:n, lcs], [[0, 4], [1, 1]]), ALU.add,
               ['tmpW', 'cmask'], ['tmpW'])
            act(WT[0:n, :, lcs], tmpW[0:n, :, lcs], AF.Exp, ['tmpW'], ['WT'])
        tt('dve', kwt[0:n, :, :], ktm[0:n, :].rearrange("p (h d) -> p h d", h=4),
           fap(WT[0:n, :, lcs], [[64, 4], [0, 128]]), ALU.mult, [rktm, 'WT'], ['kwt'])
        for c in range(nchunks):
            sl = slice(64 * c, 64 * c + 64)
            lc = 64 * c + lastcol
            if full:
                for h in range(4):
                    mm(hv(psN)[sl, h, 0:129], mqkT[:, h, sl], Cb[:, h, 0:129], True, True, ['mqkT', 'Cb'], ['psN'])
                    mm(hv(psAB)[sl, h, 0:129], sTt[sl, h, :], v1[sl, h, 0:129], True, True, ['sT', rv1], ['psA', 'psB'])
            for h in range(4):
                mm(psS[:, 512 + h * 128:512 + (h + 1) * 128], kwt[sl, h, :], v1[sl, h, 0:128], True, True,
                   ['kwt', rv1], ['psS1'])
                mm(psY[:, 300 + h:301 + h], kwt[sl, h, :], v1[sl, h, 128:129], True, True, ['kwt', rv1], ['psY'])
            ts('dve', dg[:, :], identf[0:4, 0:4], rowsB[:, 1, lc:lc + 1], None, ALU.mult, None, ['identf', rrows],
               ['dg'])
            mm(psY[:, 320:324], ones4[:, :], dg[:, :], True, True, ['ones4', 'dg'], ['psY'])
            cp('dve', sml[:, 0:4], psY[:, 320:324], ['psY'], ['decay'])
            tt('dve', Ctmp[:, :, 0:129], Cst[:, :, 0:129], fap(sml[:, 0:4], [[1, 4], [0, 129]]), ALU.mult,
               ['Cst', 'decay'], ['Ctmp'])
            tt('dve', Cst[:, :, 0:128], Ctmp[:, :, 0:128], psS[:, 512:1024].rearrange("p (h d) -> p h d", h=4), ALU.add,
               ['Ctmp', 'psS1'], ['Cst'])
            tt('dve', Cst[:, :, 128:129], Ctmp[:, :, 128:129], psY[:, 300:304].rearrange("p (h o) -> p h o", o=1),
               ALU.add, ['Ctmp', 'psY'], ['Cst'])
            cp('act', Cb[:, :, 0:129], Cst[:, :, 0:129], ['Cst'], ['Cb'])
        if not full:
            return
        for h in range(4):
            act(qCs[0:n, h, 0:129], hv(psN)[0:n, h, 0:129], AF.Copy, ['psN', rcols], ['qCs'], scale=cols[0:n, 8 + h:9 + h])
        tt('dve', nd[0:n, :, 0:129], qCs[0:n, :, 0:129], hv(psAB)[0:n, :, 0:129], ALU.add, ['qCs', 'psA', 'psB'], ['nd'])
        stt('dve', sml[0:n, 4:8], nd[0:n, :, 128], -1.0, nd[0:n, :, 128], ALU.mult, ALU.max, ['nd'], ['den'])
        tt('dve', sml[0:n, 4:8], sml[0:n, 4:8], cols[0:n, 4:8], ALU.max, ['den', rcols], ['den'])
        recip(sml[0:n, 4:8], ['den'], ['den'])
        tt('dve', hm[0:n, :, :], nd[0:n, :, 0:128], fap(sml[0:n, 4:8], [[1, 4], [0, 128]]), ALU.mult, ['nd', 'den'], ['hm'])
        tt('pool', hm[0:n, :, :], hm[0:n, :, :], ogs[0:n, :].rearrange("p (h d) -> p h d", h=4), ALU.mult,
           ['hm', 'ogs'], ['hm'])
        hsq = tmk[:, 512:1024].rearrange("p (h d) -> p h d", h=4)
        tt('pool', hsq[0:n, :, :], hm[0:n, :, :], hm[0:n, :, :], ALU.mult, ['hm'], ['tmk512'])
        reduce(sml[0:n, 8:12], hsq[0:n, :, :], ALU.add, ['tmk512'], ['hss'])
        ts('dve', sml[0:n, 8:12], sml[0:n, 8:12], 1.0 / 128, EPS, ALU.mult, ALU.add, ['hss'], ['hss'])
        tt('pool', sml[0:n, 8:12], sml[0:n, 8:12], chalf[0:n, 0:4], ALU.pow, ['hss', 'chalf'], ['hss'])
        tt('dve', hm[0:n, :, :], hm[0:n, :, :], fap(sml[0:n, 8:12], [[1, 4], [0, 128]]), ALU.mult, ['hm', 'hss'], ['hm'])
        tt('dve', aout[0:n, 512:1024].rearrange("p (h d) -> p h d", h=4), hm[0:n, :, :],
           normm_t[0:n, :].rearrange("p (h d) -> p h d", h=4), ALU.mult,
           ['hm', 'normm'], ['aout_m'])

    psAB_bf = psAB.bitcast(BF16)

    def attention_tile(slots, vm, halo_cols):
        def bufs(h):
            p = h % 2
            SC = (psS, psN)[p]
            rsc = (['psS0', 'psS1'], ['psN'])[p]
            pst = (psT, psAB_bf)[p]
            rpst = (['psT'], ['psA', 'psB'])[p]
            return p, SC, rsc, pst, rpst

        def scores(h):
            p, SC, rsc, pst, rpst = bufs(h)
            po = 64 * (h % 2)
            blk = h // 2
            mm(SC[:, 0:512], identb[:, :], vmask_b[:, vm, 0:512], True, False, ['identb', 'vmaskb'], rsc)
            mm(SC[:, 512:640], identb[:, :], vmask_b[:, vm, 512:640], True, False, ['identb', 'vmaskb'], rsc)
            for j, s_ in enumerate(slots):
                mm(SC[:, j * 128:(j + 1) * 128], qT[po:po + 64, blk, :], KT[po:po + 64, blk, s_ * 128:(s_ + 1) * 128],
                   False, True, ['qT', 'KT'], rsc)

        def softmax(h):
            p, SC, rsc, pst, rpst = bufs(h)
            S, Pm = S_sbs[p], P_sbs[p]
            c0 = 16 + 4 * p
            tt('dve', S[:, :], SC[:, 0:640], bias_t[:, h, :], ALU.add, rsc + A, ['S_sb%d' % p])
            if halo_cols > 0:
                ts('dve', S[:, 0:halo_cols], S[:, 0:halo_cols], halom_t[:, 0:1], None, ALU.add, None,
                   ['S_sb%d' % p] + A, ['S_sb%d' % p])
            reduce(sml[:, c0:c0 + 1], S[:, :], ALU.max, ['S_sb%d' % p], ['amax%d' % p])
            ts('dve', sml[:, c0:c0 + 1], sml[:, c0:c0 + 1], -1.0, None, ALU.mult, None, ['amax%d' % p], ['amax%d' % p])
            memset('pool', sml[:, c0 + 1:c0 + 2], 0.0, ['asum%d' % p])
            act(Pm[:, :], S[:, :], AF.Exp, ['S_sb%d' % p, 'amax%d' % p, 'asum%d' % p], ['P_sb%d' % p, 'asum%d' % p],
                bias=sml[:, c0:c0 + 1], accum=sml[:, c0 + 1:c0 + 2])

        def transpose_pv(h):
            p, SC, rsc, pst, rpst = bufs(h)
            Pm, PT = P_sbs[p], PT_sbs[p]
            for j in range(5):
                tr(pst[:, j * 128:(j + 1) * 128], Pm[:, j * 128:(j + 1) * 128], identb[:, :], ['P_sb%d' % p, 'identb'], rpst)
            cp('act', PT[:, :], pst[:, 0:640], rpst, ['PT_sb%d' % p])
            for j, s_ in enumerate(slots):
                mm(SC[:, 640:704], PT[:, j * 128:(j + 1) * 128], VR[:, s_, h * 64:(h + 1) * 64], j == 0, j == 4,
                   ['PT_sb%d' % p, 'VR'], rsc)

        def finish(h):
            p, SC, rsc, pst, rpst = bufs(h)
            c0 = 16 + 4 * p
            recip(sml[:, c0 + 1:c0 + 2], ['asum%d' % p], ['asum%d' % p])
            ts('dve', aout[:, h * 64:(h + 1) * 64], SC[:, 640:704], sml[:, c0 + 1:c0 + 2], None, ALU.mult, None,
               rsc + ['asum%d' % p], ['aout_a'])

        scores(0)
        softmax(0)
        for h in range(8):
            if h + 1 < 8:
                scores(h + 1)
            transpose_pv(h)
            if h + 1 < 8:
                softmax(h + 1)
            finish(h)

    def mixer_tile(k, ti, mode, slot, sidx=None, par=0, part='all'):
        ktm, v1 = ktms[par], v1s[par]
        xt = xts[k]
        xres = 'xt%d' % k
        rmsnorm_to_hT(xt, xres)
        is_full = mode in ('full', 'sample')
        nchunks = 1 if mode == 'sample' else 2
        lastcol = 31 if mode == 'sample' else 63
        if mode in ('halo', 'full', 'sample'):
            def ev_k(b0, nb):
                cp('act', KT[:, b0:b0 + nb, slot * 128:(slot + 1) * 128],
                   psA[:, 0:nb * 128].rearrange("p (b t) -> p b t", b=nb), ['psA'], ['KT'])
            project_fm(OFF_AK, 4, ev_k)
            project_tm(OFF_AV, lambda: cp('act', VR[:, slot, :], psY[:, :], ['psY'], ['VR']))
        if mode == 'halo':
            if slot == 3:
                def ev_h(b0, nb):
                    cp('act', uT[:, b0:b0 + nb, 3:131], psA[:, 0:nb * 128].rearrange("p (b t) -> p b t", b=nb),
                       ['psA'], ['uT'])
                project_fm(OFF_MQK, 8, ev_h)
                cp('pool', uT[:, :, 0:3], uT[:, :, 128:131], ['uT'], ['uT'])
            return
        if mode == 'lite':
            def ev_u(b0, nb):
                cp('act', uT[:, 4 + b0:4 + b0 + nb, 3:131], psA[:, 0:nb * 128].rearrange("p (b t) -> p b t", b=nb),
                   ['psA'], ['uT'])
            project_fm(OFF_MQK + 512, 4, ev_u)
            conv_silu(4, 8)
        else:
            def ev_q(b0, nb):
                act(qT[:, b0:b0 + nb, :], psA[:, 0:nb * 128].rearrange("p (b t) -> p b t", b=nb), AF.Copy, ['psA'], ['qT'],
                    scale=0.125)
            project_fm(0, 4, ev_q)

            def ev_u(b0, nb):
                cp('act', uT[:, b0:b0 + nb, 3:131], psA[:, 0:nb * 128].rearrange("p (b t) -> p b t", b=nb), ['psA'], ['uT'])
            project_fm(OFF_MQK, 8, ev_u)
            conv_silu(0, 8)
            def ev_og():
                act(ogs[:, :], psY[:, :], AF.Exp, ['psY'], ['ogs'], scale=-1.0)
                act(ogs[:, :], ogs[:, :], AF.Ln, ['ogs'], ['ogs'], bias=1.0)
                act(ogs[:, :], ogs[:, :], AF.Exp, ['ogs'], ['ogs'], scale=-1.0)
            project_tm(OFF_MO, ev_og)
        project_tm(OFF_MV, lambda: cp('act', v1[:, :, 0:128], psY[:, :].rearrange("p (h d) -> p h d", h=4), ['psY'], ['v1_%d' % par]))
        for h in range(4):
            tr(psT[:, h * 128:(h + 1) * 128], mqkT[:, 4 + h, :], identb[:, :], ['mqkT', 'identb'], ['psT'])
        cp('act', ktm[:, :], psT[:, 0:512], ['psT'], ['ktm%d' % par])
        gates_rows(nchunks, lastcol, par)
        if mode == 'full':
            pi = ti - NPRE
            attention_tile([(slot + 1 + j) % 5 for j in range(5)], 0, max(0, 512 - 128 * pi))
        elif mode == 'sample':
            attention_tile([0, 1, 2, 3, 4], 1, 0)
        if part == 'front':
            return
        mlstm_tile(nchunks, lastcol, is_full, par)
        if mode == 'lite':
            return
        if mode == 'sample' or ti - NPRE >= NP - 4:
            ev = lambda: cp('act', tmk[:, 0:512], psY[:, :], ['psY'], ['tmk0'])
            for (col, o_s, o_p, nm) in [(OFF_AK, ks_o, kp_o, 'k'), (OFF_AV, vs_o, vp_o, 'v')]:
                project_tm(col, ev)
                if mode == 'sample':
                    out_ops.append(dma('sp', o_s[sidx * 32:(sidx + 1) * 32, :], tmk[0:32, 0:512], ['tmk0'],
                                       ['o_s' + nm], 'o_tmk'))
                else:
                    r0 = (ti - NPRE - (NP - 4)) * 128
                    out_ops.append(dma('sp', o_p[r0:r0 + 128, :], tmk[:, 0:512], ['tmk0'], ['o_p' + nm], 'o_tmk'))
        if mode == 'sample' or ti - NPRE == NP - 1:
            for half in range(2):
                project_tm(OFF_MQK + 512 * half, lambda: cp('act', tmk[:, 0:512], psY[:, :], ['psY'], ['tmk0']))
                if mode == 'sample':
                    out_ops.append(dma('sp', convs_o[sidx, :, half * 512:(half + 1) * 512], tmk[29:32, 0:512], ['tmk0'],
                                       ['o_cs'], 'o_tmk'))
                else:
                    out_ops.append(dma('sp', convp_o[:, half * 512:(half + 1) * 512], tmk[125:128, 0:512], ['tmk0'],
                                       ['o_cp'], 'o_tmk'))
        for c in range(8):
            tr(psT[:, c * 128:(c + 1) * 128], aout[:, c * 128:(c + 1) * 128], identb[:, :], ['aout_a', 'aout_m', 'identb'],
               ['psT'])
        cp('act', mixT[:, :, :], psT[:, :].rearrange("p (k t) -> p k t", k=8), ['psT'], ['mixT'])
        for half in range(2):
            for kc in range(8):
                mm(psY[:, :], mixT[:, kc, :], woutv[:, kc, half * 512:(half + 1) * 512], kc == 0, kc == 7,
                   A + ['mixT'], ['psY'])
            tt('dve', xt[:, half * 512:(half + 1) * 512], psY[:, :], xt[:, half * 512:(half + 1) * 512], ALU.add,
               ['psY', xres], [xres])
        oi = ti - NPRE
        dma('sp', x2_d[oi * 128:(oi + 1) * 128, :], xt[:, :], [xres], ['x2_%d' % oi], xres + 's')

    def mload(ti):
        return lambda: xload(x1_d[ti * 128:(ti + 1) * 128, :], 'x1_%d' % ti)

    def reset_state(q):
        ts('dve', Cst[:, :, 0:129], Cst[:, :, 0:129], flg_t[:, q:q + 1], None, ALU.mult, None, ['Cst', 'wA'], ['Cst'])
        cp('act', Cb[:, :, 0:129], Cst[:, :, 0:129], ['Cst'], ['Cb'])
        ts('dve', rsm[:, 8:9], rsm[:, 8:9], flg_t[0:4, q:q + 1], None, ALU.mult, None, ['m_in', 'wA'], ['m_in'])
        ts('dve', uT[:, :, 0:3], uT[:, :, 0:3], flg_t[:, q:q + 1], None, ALU.mult, None, ['uT', 'wA'], ['uT'])

    def sample_tiles():
        for s in range(4):
            ti = NPRE + NP + s
            dma('pool', ckl[:, :, :], ck_d[s].rearrange("(kt p) f -> p kt f", p=128), (), ['ckl'] + A, 'ckl')
            dma('pool', VR[:, 0:4, :], cv_d[s].rearrange("(kt p) f -> p kt f", p=128), (), ['VR'] + A, 'cvl')
            for kt in range(4):
                for b in range(4):
                    tr(psT[:, b * 128:(b + 1) * 128], ckl[:, kt, b * 128:(b + 1) * 128], identb[:, :], ['ckl', 'identb'],
                       ['psT'])
                cp('act', KT[:, :, kt * 128:(kt + 1) * 128], psT[:, 0:512].rearrange("p (b t) -> p b t", b=4), ['psT'],
                   ['KT'])
            dma('sp', Cst[:, :, 0:128], sC_d[s].rearrange("h p d -> p h d"), (), ['Cst'] + A, 'sCl')
            dma('sp', Cst[:, :, 128:129], bass.AP(sn_d, s * 512, [[1, 128], [128, 4], [1, 1]]), (), ['Cst'] + A, 'snl',
                slow=True)
            dma('sp', rsm[:, 8:9], bass.AP(sm_d, s * 4, [[1, 4], [1, 1]]), (), ['m_in'] + A, 'sml', slow=True)
            cp('act', Cb[:, :, 0:129], Cst[:, :, 0:129], ['Cst'], ['Cb'])
            for r_ in range(3):
                dma('sp', uT[:, :, r_], bass.AP(sconv_d, (s * 3 + r_) * D, [[1, 128], [128, 8]]), (), ['uT'] + A,
                    'scl%d' % r_, slow=True)
            xload(x1_d[ti * 128:(ti + 1) * 128, :], 'x1_%d' % ti)
            mixer_tile(pend.pop(0), ti, 'sample', 4, sidx=s)
            out_ops.append(dma('sp', Cs_o[s].rearrange("h p d -> p h d"), Cst[:, :, 0:128], ['Cst'], ['o_Cs'], 'o_C'))
            out_ops.append(dma('sp', bass.AP(ns_o, s * 512, [[1, 128], [128, 4], [1, 1]]), Cst[:, :, 128:129], ['Cst'],
                               ['o_ns'], 'o_n', slow=True))
            out_ops.append(dma('sp', bass.AP(ms_o, s * 4, [[1, 4], [1, 1]]), rsm[:, 8:9], ['m_in'], ['o_ms'], 'o_m',
                               slow=True))

    jobsM = []
    if MODE_X:
        memset('pool', rsm[:, 8:9], -1.0e30, ['m_in'], ['m_in'])
        jobsB = [(mload(3), lambda k: mixer_tile(k, 3, 'halo', 3))]
        for pi in range(NP):
            ti = NPRE + pi
            jobsB.append((mload(ti), lambda k, ti=ti: mixer_tile(k, ti, 'lite', 0)))
        run_jobs(jobsB)
        dma('sp', sum_in[0:512, :].rearrange("(h p) c -> p h c", p=128), Cst[:, :, :], ['Cst'], ['dram_sum'], 'sumC')
        dma('sp', sum_in[512:516, 0:2], rsm[:, 8:10], ['m_in'], ['dram_sum'], 'sumM')
        P.op('pool', lambda e: e.collective_compute("AllGather", ALU.bypass, replica_groups=[list(range(NCORES))],
                                                    ins=[sum_in.ap().opt()], outs=[sum_all.ap().opt()]),
             r=['dram_sum'], w=['dram_all'], cc=True)
        sample_tiles()
        for r_ in range(NCORES):
            dma('sp', scal[:, r_, :, :], bass.AP(sum_all, (r_ * SR + 512) * 132, [[0, 128], [132, 4], [1, 2]]),
                ['dram_all'], ['scal'] + A, 'scal')
        memset('pool', Cst[:, :, :], 0.0, ['Cst'], ['Cst'])
        memset('pool', fold[:, 0:4], 0.0, ['fold_m'], A)
        F = lambda i: fold[:, 4 * i:4 * i + 4]
        for r_ in range(NCORES):
            dma('sp', Csum[:, :, :], sum_all[r_ * SR:r_ * SR + 512, :].rearrange("(h p) c -> p h c", p=128),
                ['dram_all'], ['Csum'] + A, 'Csum')
            e_ = fle_t[:, r_:r_ + 1]
            ts('dve', F(1), scal[:, r_, :, 1], e_, None, ALU.mult, None, ['scal'] + A, ['f1'])
            ts('dve', F(2), scal[:, r_, :, 0], e_, flm_t[:, r_:r_ + 1], ALU.mult, ALU.add, ['scal'] + A, ['f2'])
            tt('dve', F(1), F(1), F(0), ALU.add, ['f1', 'fold_m'], ['f1'])
            tt('dve', F(3), F(1), F(2), ALU.max, ['f1', 'f2'], ['f3'])
            tt('dve', F(1), F(1), F(3), ALU.subtract, ['f1', 'f3'], ['f1'])
            tt('dve', F(2), F(2), F(3), ALU.subtract, ['f2', 'f3'], ['f2'])
            act(F(1), F(1), AF.Exp, ['f1'], ['f1'])
            act(F(2), F(2), AF.Exp, ['f2'], ['f2'])
            tt('dve', Cst[:, :, :], Cst[:, :, :], fap(fold[:, 4:8], [[1, 4], [0, 132]]), ALU.mult, ['Cst', 'f1'], ['Cst'])
            tt('dve', Csum[:, :, :], Csum[:, :, :], fap(fold[:, 8:12], [[1, 4], [0, 132]]), ALU.mult, ['Csum', 'f2'],
               ['Csum'])
            tt('dve', Cst[:, :, :], Cst[:, :, :], Csum[:, :, :], ALU.add, ['Cst', 'Csum'], ['Cst'])
            cp('dve', F(0), F(3), ['f3'], ['fold_m'])
        cp('act', Cb[:, :, 0:129], Cst[:, :, 0:129], ['Cst'], ['Cb'])
        tt('dve', fold[0:4, 16:20], fold[0:4, 0:4], identf[0:4, 0:4], ALU.mult, ['fold_m', 'identf'], ['f4'])
        reduce(rsm[:, 8:9], fold[0:4, 16:20], ALU.add, ['f4'], ['m_in'])
    else:
        lite = [(q, pi) for q in range(3) for pi in range(NP)]

        uviews = [uT[:, 4:8, :], uTb[:, :, :]]

        def lF1(i):
            q, pi = lite[i]
            ti = q * NP + pi
            par = i % 2
            uv, up = uviews[par], uviews[1 - par]
            ru, rup = 'uTk%d' % par, 'uTk%d' % (1 - par)
            xload(x1_d[ti * 128:(ti + 1) * 128, :], 'x1_%d' % ti)
            k = pend.pop(0)
            rmsnorm_to_hT(xts[k], 'xt%d' % k)

            def ev_u(b0, nb):
                cp('act', uv[:, b0:b0 + nb, 3:131], psA[:, 0:nb * 128].rearrange("p (b t) -> p b t", b=nb), ['psA'], [ru])
            project_fm(OFF_MQK + 512, 4, ev_u)
            cp('pool', uv[:, :, 0:3], up[:, :, 128:131], [rup], [ru])
            if pi == 0 and q > 0:
                ts('dve', uv[:, :, 0:3], uv[:, :, 0:3], flg_t[:, q - 1:q], None, ALU.mult, None, [ru] + A, [ru])
            v1 = v1s[i % 3]
            project_tm(OFF_MV, lambda: cp('act', v1[:, :, 0:128], psY[:, :].rearrange("p (h d) -> p h d", h=4), ['psY'],
                                          ['v1_%d' % (i % 3)]))
            gates_rows(2, 63, par, part='A')

        def lF2(i):
            q, pi = lite[i]
            par = i % 2
            conv_part('dve', 4, 8, 512, uv=uviews[par], ru='uTk%d' % par)
            ts('dve', mqkT[:, 4:8, :], mqkT[:, 4:8, :], 128 ** -0.5, None, ALU.mult, None, ['mqkT'], ['mqkT'])
            for h in range(4):
                tr(psT[:, h * 128:(h + 1) * 128], mqkT[:, 4 + h, :], identb[:, :], ['mqkT', 'identb'], ['psT'])
            cp('act', ktms[par][:, :], psT[:, 0:512], ['psT'], ['ktm%d' % par])
            gates_rows(2, 63, par, part='B')
            if pi == NP - 1:
                ts('dve', rsm[:, 8:9], rsm[:, 8:9], flg_t[0:4, q:q + 1], None, ALU.mult, None, ['m_in'] + A, ['m_in'])

        def lback(i):
            q, pi = lite[i]
            mlstm_tile(2, 63, False, i % 2, vpar=i % 3)
            if pi == NP - 1:
                ts('dve', Cst[:, :, 0:129], Cst[:, :, 0:129], flg_t[:, q:q + 1], None, ALU.mult, None, ['Cst'] + A,
                   ['Cst'])
                cp('act', Cb[:, :, 0:129], Cst[:, :, 0:129], ['Cst'], ['Cb'])
        nl = len(lite)
        lF1(0)
        lF1(1)
        lF2(0)
        for i in range(nl):
            if i + 2 < nl:
                lF1(i + 2)
            if i + 1 < nl:
                lF2(i + 1)
            lback(i)
    for j in range(4):
        ti = NPRE - 4 + j
        jobsM.append((mload(ti), lambda k, ti=ti, j=j: mixer_tile(k, ti, 'halo', j)))
    for pi in range(NP):
        ti = NPRE + pi
        jobsM.append((mload(ti), lambda k, ti=ti, pi=pi: mixer_tile(k, ti, 'full', (4 + pi) % 5)))
    run_jobs(jobsM)
    out_ops.append(dma('sp', Cp_o[:, :, :].rearrange("h p d -> p h d"), Cst[:, :, 0:128], ['Cst'], ['o_Cp'], 'o_C'))
    out_ops.append(dma('sp', bass.AP(np_o, 0, [[1, 128], [128, 4], [1, 1]]), Cst[:, :, 128:129], ['Cst'], ['o_np'], 'o_n',
                       slow=True))
    out_ops.append(dma('sp', mp_o[:, :], rsm[:, 8:9], ['m_in'], ['o_mp'], 'o_m', slow=True))
    if not MODE_X:
        sample_tiles()

    P.barrier()
    load_gain(2)
    load_ffn_weights("w1b", "w3b", "w2b")
    ffn_phase([(x2_d[i * 128:(i + 1) * 128, :], 'x2_%d' % i, y_o[i * 128:(i + 1) * 128, :], 'o_y')
               for i in range(NOUT)], True, out_ops)

    nsem, maxcnt = P.emit(out_ops)
    info = dict(nops=len(P.ops), nsem=nsem, maxcnt=maxcnt, sbuf_used=sbuf_used)
    return nc, info


MODE_X = False


def _prep_inputs(inp, NP):
    S = 4 * NP * 128
    xp = np.asarray(inp['x_prompt'], np.float32)
    xs = np.asarray(inp['x_sample'], np.float32)
    NPRE = 4 if MODE_X else 3 * NP
    TT = NPRE + NP + 4
    ii = np.arange(128)[:, None]
    jj = np.arange(640)[None, :]
    idx = np.clip(ii + 512 - jj, -128, 128) + 128
    rb = np.asarray(inp['rel_bias'], np.float32)[0]
    relexp = np.ascontiguousarray(np.transpose(rb[:, idx], (1, 0, 2))).reshape(128, 8 * 640)
    vmask = np.full((2, 128, 640), MASKV, np.float32)
    vmask[0][(ii < 64) & (jj < 576) | (ii >= 64) & (jj >= 64)] = 0.0
    vmask[1][:, :544] = 0.0
    cmask = np.where((np.arange(128)[:, None] % 64) <= np.arange(64)[None, :], 0.0, MASKV).astype(np.float32)
    ident = np.eye(128, dtype=np.float32)
    sel = np.zeros((4, 4, 128), np.float32)
    for h in range(4):
        sel[h, h, :] = 1.0
    cwv = np.asarray(inp['conv_w'], np.float32)[0]
    cw = np.ascontiguousarray(cwv.reshape(4, 8, 128).transpose(2, 1, 0)).reshape(128, 32)
    cb = np.ascontiguousarray(np.asarray(inp['conv_b'], np.float32)[0].reshape(8, 128).T)
    gb = np.ascontiguousarray(np.asarray(inp['gate_bias'], np.float32)[0].reshape(2, 4).T)
    gains = np.stack([np.asarray(inp['norm_ffn1'], np.float32)[0], np.asarray(inp['norm_mix'], np.float32)[0],
                      np.asarray(inp['norm_ffn2'], np.float32)[0], np.asarray(inp['norm_final'], np.float32)])
    shared = dict(
        w1a=np.asarray(inp['w1_ffn1'], np.float32)[0], w3a=np.asarray(inp['w3_ffn1'], np.float32)[0],
        w2a=np.asarray(inp['w2_ffn1'], np.float32)[0], w1b=np.asarray(inp['w1_ffn2'], np.float32)[0],
        w3b=np.asarray(inp['w3_ffn2'], np.float32)[0], w2b=np.asarray(inp['w2_ffn2'], np.float32)[0],
        w_in=np.asarray(inp['w_in'], np.float32)[0], w_out=np.asarray(inp['w_out'], np.float32)[0],
        gains=gains, normm=np.asarray(inp['norm_mlstm_out'], np.float32), cw=cw, cb=cb, gb=gb,
        relexp=relexp, vmask=vmask, cmask=cmask, ident=ident, sel=sel.reshape(4, 512))
    ck = np.asarray(inp['cache_attn_k'], np.float32)[0].reshape(32, 512, 512)
    cv = np.asarray(inp['cache_attn_v'], np.float32)[0].reshape(32, 512, 512)
    sC = np.asarray(inp['state_mlstm_C'], np.float32)[0]
    sn = np.asarray(inp['state_mlstm_n'], np.float32)[0]
    sm = np.asarray(inp['state_mlstm_m'], np.float32)[0]
    sconv = np.asarray(inp['state_mlstm_conv'], np.float32)[0]
    maps = []
    L = NP * 128
    for c in range(NCORES):
        b, seg = c // 4, c % 4
        xin = np.zeros((TT * 128, D), np.float32)
        flg = np.zeros((128, 4), np.float32)
        fle = np.zeros((128, 8), np.float32)
        for r_ in range(NCORES):
            if r_ // 4 == b and r_ % 4 < seg:
                fle[:, r_] = 1.0
        flm = ((1.0 - fle) * -1.0e30).astype(np.float32)
        if MODE_X:
            if seg > 0:
                xin[0:512] = xp[b, seg * L - 512:seg * L]
        else:
            for q in range(3):
                src_seg = seg - 3 + q
                if src_seg >= 0:
                    xin[q * L:(q + 1) * L] = xp[b, src_seg * L:(src_seg + 1) * L]
                    flg[:, q] = 1.0
        xin[NPRE * 128:NPRE * 128 + L] = xp[b, seg * L:(seg + 1) * L]
        for s in range(4):
            r0 = (NPRE + NP + s) * 128
            xin[r0:r0 + 32] = xs[4 * c + s]
        m = dict(shared)
        m.update(xin=xin, flg=flg, fle=fle, flm=flm, halom=np.full((128, 1), MASKV if seg == 0 else 0.0, np.float32),
                 ck=np.ascontiguousarray(ck[4 * c:4 * c + 4]), cv=np.ascontiguousarray(cv[4 * c:4 * c + 4]),
                 sC=np.ascontiguousarray(sC[4 * c:4 * c + 4]), sn=np.ascontiguousarray(sn[4 * c:4 * c + 4]),
                 sm=np.ascontiguousarray(sm[4 * c:4 * c + 4]), sconv=np.ascontiguousarray(sconv[4 * c:4 * c + 4]))
        maps.append(m)
    return maps


def _assemble(res, NP):
    L = NP * 128
    S = 4 * L
    f = np.float32
    y_p = np.zeros((2, S, D), f)
    y_s = np.zeros((32, 32, D), f)
    kp = np.zeros((1, 2, 512, 8, 64), f)
    vp = np.zeros((1, 2, 512, 8, 64), f)
    Cp = np.zeros((1, 2, 4, 128, 128), f)
    npp = np.zeros((1, 2, 4, 128), f)
    mp = np.zeros((1, 2, 4), f)
    convp = np.zeros((1, 2, 3, D), f)
    ks = np.zeros((1, 32, 32, 8, 64), f)
    vs = np.zeros((1, 32, 32, 8, 64), f)
    Cs = np.zeros((1, 32, 4, 128, 128), f)
    ns = np.zeros((1, 32, 4, 128), f)
    ms = np.zeros((1, 32, 4), f)
    convs = np.zeros((1, 32, 3, D), f)
    for c in range(NCORES):
        r = res[c]
        b, seg = c // 4, c % 4
        y_p[b, seg * L:(seg + 1) * L] = r['y'][0:L]
        for s in range(4):
            q = 4 * c + s
            y_s[q] = r['y'][(NP + s) * 128:(NP + s) * 128 + 32]
            ks[0, q] = r['ks'][s * 32:(s + 1) * 32].reshape(32, 8, 64)
            vs[0, q] = r['vs'][s * 32:(s + 1) * 32].reshape(32, 8, 64)
            Cs[0, q] = r['Cs'][s]
            ns[0, q] = r['ns'][s]
            ms[0, q] = r['ms'][s]
            convs[0, q] = r['convs'][s]
        if seg == 3:
            kp[0, b] = r['kp'].reshape(512, 8, 64)
            vp[0, b] = r['vp'].reshape(512, 8, 64)
            Cp[0, b] = r['Cp']
            npp[0, b] = r['npo']
            mp[0, b] = r['mpo'][:, 0]
            convp[0, b] = r['convp']
    return (y_p, y_s, kp, vp, Cp, npp, mp, convp, ks, vs, Cs, ns, ms, convs)


def run(inputs, NP):
    nc, info = build(NP)
    maps = _prep_inputs(inputs, NP)
    res = run_bass_kernel_spmd(nc, maps, core_ids=list(range(NCORES)))
    return _assemble(res.results, NP), info


def kernel(**inputs):
    import jax
    jax.devices("cpu")
    out, _ = run(inputs, 32)
    return out
```
